# Optimizing a Trainium2 kernel written in Bass

```python
import math
import jax, jax.numpy as jnp
from jax import lax
import numpy as np

D_MODEL = 1024
BATCH = 2
SEQ = 8192
DEPTH = 4

N_A_LAYERS = DEPTH // 2
N_B_LAYERS = DEPTH - N_A_LAYERS
EPS = 1e-6

A_HEADS = 8
A_DK = 128
A_DV = 256
A_QK_W = A_HEADS * A_DK
A_V_W = A_HEADS * A_DV
A_CONV_CH = 2 * A_QK_W + A_V_W
A_IN_W = A_CONV_CH + A_V_W + 2 * A_HEADS
CONV_W = 4
CHUNK = 64

B_Q_HEADS = 32
B_KV_HEADS = 4
B_GROUP = B_Q_HEADS // B_KV_HEADS
B_HD = 64
B_W = B_Q_HEADS * B_HD
B_KV_W = B_KV_HEADS * B_HD
WINDOW = 128
BLOCK = 128

N_BUCKETS = 32
MAX_DIST = 128

kernel_name = "yoco_gated_deltanet_swa_sink_hybrid"


def rms_norm(x, g):
    xf = x.astype(jnp.float32)
    y = xf * lax.rsqrt(jnp.mean(xf * xf, axis=-1, keepdims=True) + EPS)
    return (y * g.astype(jnp.float32)).astype(x.dtype)


def l2_norm(x):
    xf = x.astype(jnp.float32)
    return xf * lax.rsqrt(jnp.sum(xf * xf, axis=-1, keepdims=True) + EPS)


def causal_depthwise_conv(x, w):
    c = x.shape[-1]
    return lax.conv_general_dilated(
        x, w[:, None, :].astype(x.dtype), window_strides=(1,),
        padding=[(CONV_W - 1, 0)], dimension_numbers=("NWC", "WIO", "NWC"),
        feature_group_count=c)


def chunk_gated_delta_rule(q, k, v, g, beta):
    B, T, H, _ = q.shape
    N = T // CHUNK

    def chunks(t):
        return t.reshape(B, N, CHUNK, H, -1).transpose(0, 1, 3, 2, 4)

    q, k, v = chunks(q), chunks(k), chunks(v)
    g = jnp.cumsum(g.reshape(B, N, CHUNK, H).transpose(0, 1, 3, 2), axis=-1)
    beta = beta.reshape(B, N, CHUNK, H).transpose(0, 1, 3, 2)

    tri = jnp.tril(jnp.ones((CHUNK, CHUNK), dtype=bool))
    strict = jnp.tril(jnp.ones((CHUNK, CHUNK), dtype=bool), -1)
    decay = jnp.exp(jnp.where(tri, g[..., :, None] - g[..., None, :], -jnp.inf))

    kk = jnp.einsum("bnhcd,bnhsd->bnhcs", k, k)
    lower = jnp.where(strict, beta[..., :, None] * kk * decay, 0.0)
    rhs = jnp.concatenate([v * beta[..., None], k * (beta * jnp.exp(g))[..., None]], axis=-1)
    sol = lax.linalg.triangular_solve(lower, rhs, left_side=True, lower=True, unit_diagonal=True)
    u, w = sol[..., :A_DV], sol[..., A_DV:]

    qk = jnp.einsum("bnhcd,bnhsd->bnhcs", q, k) * decay
    q_dec = q * jnp.exp(g)[..., None]
    k_dec = k * jnp.exp(g[..., -1:] - g)[..., None]
    g_last = jnp.exp(g[..., -1])

    def step(S, inp):
        qk_i, qd_i, kd_i, u_i, w_i, gl_i = inp
        v_new = u_i - jnp.einsum("bhck,bhkv->bhcv", w_i, S)
        o = jnp.einsum("bhck,bhkv->bhcv", qd_i, S) + jnp.einsum("bhcs,bhsv->bhcv", qk_i, v_new)
        S = S * gl_i[..., None, None] + jnp.einsum("bhck,bhcv->bhkv", kd_i, v_new)
        return S, o

    xs = tuple(t.swapaxes(0, 1) for t in (qk, q_dec, k_dec, u, w, g_last))
    S0 = jnp.zeros((B, H, A_DK, A_DV), jnp.float32)
    _, o = lax.scan(step, S0, xs)
    return o.transpose(1, 0, 3, 2, 4).reshape(B, T, H, A_DV)


def gated_deltanet_mixer(h, w_in, conv_w, a_log, dt_bias, o_gain, w_out):
    B, T, _ = h.shape
    proj = h @ w_in
    qkv, z, ab = jnp.split(proj, [A_CONV_CH, A_CONV_CH + A_V_W], axis=-1)
    qkv = jax.nn.silu(causal_depthwise_conv(qkv, conv_w))
    q, k, v = jnp.split(qkv, [A_QK_W, 2 * A_QK_W], axis=-1)
    q = l2_norm(q.reshape(B, T, A_HEADS, A_DK)) * (A_DK ** -0.5)
    k = l2_norm(k.reshape(B, T, A_HEADS, A_DK))
    v = v.reshape(B, T, A_HEADS, A_DV).astype(jnp.float32)
    b_logit, a_logit = jnp.split(ab.astype(jnp.float32), 2, axis=-1)
    beta = jax.nn.sigmoid(b_logit)
    g = -jnp.exp(a_log.astype(jnp.float32)) * jax.nn.softplus(a_logit + dt_bias.astype(jnp.float32))
    o = chunk_gated_delta_rule(q, k, v, g, beta)
    o = rms_norm(o, o_gain) * jax.nn.silu(z.reshape(B, T, A_HEADS, A_DV).astype(jnp.float32))
    return o.reshape(B, T, A_V_W).astype(h.dtype) @ w_out


def shared_kv(x, kv_norm, w_kv, k_gain):
    B, T, _ = x.shape
    nb = T // BLOCK
    kv = rms_norm(x, kv_norm) @ w_kv
    k, v = jnp.split(kv, 2, axis=-1)
    k = rms_norm(k.reshape(B, T, B_KV_HEADS, B_HD), k_gain)
    v = v.reshape(B, T, B_KV_HEADS, B_HD)

    def band(t):
        tb = t.reshape(B, nb, BLOCK, B_KV_HEADS, B_HD)
        prev = jnp.concatenate([jnp.zeros_like(tb[:, :1]), tb[:, :-1]], axis=1)
        return jnp.concatenate([prev, tb], axis=2)

    return band(k), band(v)


def t5_causal_bucket(dist):
    n = np.maximum(dist, 0)
    max_exact = N_BUCKETS // 2
    large = max_exact + (np.log(np.maximum(n, 1) / max_exact) / np.log(MAX_DIST / max_exact)
                         * (N_BUCKETS - max_exact)).astype(np.int64)
    large = np.minimum(large, N_BUCKETS - 1)
    return np.where(n < max_exact, n, large).astype(np.int32)


def band_bias_and_mask(rel_bias, seq):
    nb = seq // BLOCK
    qi = np.arange(BLOCK)[:, None]
    s = np.arange(2 * BLOCK)[None, :]
    dist = qi + BLOCK - s
    in_window = (dist >= 0) & (dist < WINDOW)
    bias = rel_bias.astype(jnp.float32)[jnp.asarray(t5_causal_bucket(dist))]
    bias = bias.transpose(2, 0, 1).reshape(B_KV_HEADS, B_GROUP, BLOCK, 2 * BLOCK)
    key_pos = np.arange(nb)[:, None] * BLOCK - BLOCK + s
    mask = in_window[None] & (key_pos >= 0)[:, None, :]
    return bias, jnp.asarray(mask)


def swa_sink_mixer(h, w_in, q_gain, sinks, w_out, k_band, v_band, bias, mask):
    B, T, _ = h.shape
    nb = T // BLOCK
    q, z = jnp.split(h @ w_in, 2, axis=-1)
    q = rms_norm(q.reshape(B, T, B_Q_HEADS, B_HD), q_gain) * (B_HD ** -0.5)
    q = q.reshape(B, nb, BLOCK, B_KV_HEADS, B_GROUP, B_HD)
    logits = jnp.einsum("bnqkgd,bnskd->bnkgqs", q, k_band).astype(jnp.float32) + bias
    logits = jnp.where(mask[None, :, None, None], logits, -jnp.inf)
    sink = sinks.astype(jnp.float32).reshape(B_KV_HEADS, B_GROUP)[..., None, None]
    m = jnp.maximum(jnp.max(logits, axis=-1, keepdims=True), sink)
    p = jnp.exp(logits - m)
    probs = p / (jnp.sum(p, axis=-1, keepdims=True) + jnp.exp(sink - m))
    o = jnp.einsum("bnkgqs,bnskd->bnqkgd", probs.astype(v_band.dtype), v_band)
    o = o.reshape(B, T, B_W) * jax.nn.silu(z)
    return o @ w_out


def setup_inputs(seed: int = 0) -> dict:
    key = jax.random.key(seed)
    ks = jax.random.split(key, 20)
    f = jnp.float32
    D = D_MODEL
    nA, nB = N_A_LAYERS, N_B_LAYERS

    def nrm(k, shape, scale):
        return jax.random.normal(k, shape, f) * scale

    dt = jnp.exp(jax.random.uniform(ks[5], (nA, A_HEADS), f, math.log(1e-3), math.log(1e-1)))
    return {
        "x": nrm(ks[0], (BATCH, SEQ, D), 1.0),
        "a_norm": 1.0 + nrm(ks[1], (nA, D), 0.02),
        "a_w_in": nrm(ks[2], (nA, D, A_IN_W), D ** -0.5),
        "a_conv": nrm(ks[3], (nA, CONV_W, A_CONV_CH), CONV_W ** -0.5),
        "a_A_log": jnp.log(jax.random.uniform(ks[4], (nA, A_HEADS), f, 1.0, 16.0)),
        "a_dt_bias": dt + jnp.log(-jnp.expm1(-dt)),
        "a_o_gain": 1.0 + nrm(ks[6], (nA, A_DV), 0.02),
        "a_w_out": nrm(ks[7], (nA, A_V_W, D), A_V_W ** -0.5),
        "kv_norm": 1.0 + nrm(ks[8], (D,), 0.02),
        "w_kv": nrm(ks[9], (D, 2 * B_KV_W), D ** -0.5),
        "k_gain": 1.0 + nrm(ks[10], (B_HD,), 0.02),
        "rel_bias": nrm(ks[11], (N_BUCKETS, B_Q_HEADS), 0.5),
        "b_norm": 1.0 + nrm(ks[12], (nB, D), 0.02),
        "b_w_in": nrm(ks[13], (nB, D, 2 * B_W), D ** -0.5),
        "b_q_gain": 1.0 + nrm(ks[14], (nB, B_HD), 0.02),
        "b_sinks": nrm(ks[15], (nB, B_Q_HEADS), 0.5),
        "b_w_out": nrm(ks[16], (nB, B_W, D), B_W ** -0.5),
    }


def reference(x, a_norm, a_w_in, a_conv, a_A_log, a_dt_bias, a_o_gain, a_w_out,
              kv_norm, w_kv, k_gain, rel_bias,
              b_norm, b_w_in, b_q_gain, b_sinks, b_w_out):
    bias, mask = band_bias_and_mask(rel_bias, x.shape[1])
    k_band = v_band = None
    for layer in range(DEPTH):
        if layer < N_A_LAYERS:
            i = layer
            x = x + gated_deltanet_mixer(rms_norm(x, a_norm[i]), a_w_in[i], a_conv[i], a_A_log[i],
                                         a_dt_bias[i], a_o_gain[i], a_w_out[i])
            if layer == N_A_LAYERS - 1:
                k_band, v_band = shared_kv(x, kv_norm, w_kv, k_gain)
        else:
            j = layer - N_A_LAYERS
            x = x + swa_sink_mixer(rms_norm(x, b_norm[j]), b_w_in[j], b_q_gain[j], b_sinks[j],
                                   b_w_out[j], k_band, v_band, bias, mask)
    return x
```

```python
import numpy as np
import ml_dtypes
from contextlib import ExitStack
import concourse.bass as bass
import concourse.mybir as mybir
from concourse.bass_utils import run_bass_kernel_spmd

F32 = mybir.dt.float32
BF16 = mybir.dt.bfloat16
AF = mybir.ActivationFunctionType
ALU = mybir.AluOpType
AX = mybir.AxisListType

D = 1024
SEQ = 8192
NB = 2
TS = 2048
EPS = 1e-6
NEG = -1.0e5


class Buf:
    __slots__ = ("name", "w", "r")

    def __init__(self, name=""):
        self.name = name
        self.w = None
        self.r = {}


class Tok:
    __slots__ = ("key", "sem", "val", "eng")

    def __init__(self, key, sem, val, eng=None):
        self.key, self.sem, self.val, self.eng = key, sem, val, eng


class Eng:
    def __init__(self, name, handle, sem, order_raw=True):
        self.name, self.h, self.sem = name, handle, sem
        self.count = 0
        self.waited = {}
        self.order_raw = order_raw
        self.ninst = 0

    def _wait(self, tok):
        if tok is None:
            return
        if tok.eng is self and not self.order_raw:
            return
        if self.waited.get(tok.key, 0) >= tok.val:
            return
        if tok.eng is not None and tok.val > tok.eng.count:
            raise RuntimeError(f"wait on pending token {tok.key} {tok.val}>{tok.eng.count} from {self.name}")
        self.h.wait_ge(tok.sem, tok.val)
        self.waited[tok.key] = tok.val

    def deps(self, reads, writes):
        for b in reads:
            if b.w is not None:
                self._wait(b.w)
        for b in writes:
            if b.w is not None and (b.w.eng is not self or self.order_raw):
                self._wait(b.w)
            for t in b.r.values():
                if t.eng is not self or self.order_raw:
                    self._wait(t)

    def op(self, ins_fn, reads=(), writes=(), sig=True):
        self.deps(reads, writes)
        ins = ins_fn()
        self.ninst += 1
        if sig:
            ins.then_inc(self.sem, 1)
            self.count += 1
            tok = Tok(self.name, self.sem, self.count, self)
        else:
            tok = Tok(self.name, self.sem, self.count + 1, self)
        for b in reads:
            b.r[self.name] = tok
        for b in writes:
            b.w = tok
            b.r = {}
        return tok


class DmaQ:
    def __init__(self, eng, sems, name):
        self.eng, self.sems, self.name = eng, sems, name
        self.n = 0
        self.counts = [0] * len(sems)
        self.toks = [None] * len(sems)

    def dma(self, out, in_, reads=(), writes=(), **kw):
        e = self.eng
        i = self.n % len(self.sems)
        self.n += 1
        e._wait(self.toks[i])
        e.deps(reads, writes)
        ins = e.h.dma_start(out=out, in_=in_, **kw)
        ins.then_inc(self.sems[i], 16)
        self.counts[i] += 16
        tok = Tok(f"{self.name}{i}", self.sems[i], self.counts[i], None)
        self.toks[i] = tok
        for b in reads:
            b.r[tok.key] = tok
        for b in writes:
            b.w = tok
            b.r = {}
        return tok


class TT:
    __slots__ = ("t", "b")

    def __init__(self, t, name):
        self.t = t
        self.b = Buf(name)


class FW:
    def __init__(self, nc, stack, ndma=8):
        self.nc = nc
        self.stack = stack
        S = lambda n: stack.enter_context(nc.semaphore(n))
        self.pe = Eng("pe", nc.tensor, S("s_pe"), order_raw=False)
        self.dve = Eng("dve", nc.vector, S("s_dve"))
        self.act = Eng("act", nc.scalar, S("s_act"))
        self.pool = Eng("pool", nc.gpsimd, S("s_pool"))
        self.sp = Eng("sp", nc.sync, S("s_sp"))
        self.q_sp = DmaQ(self.sp, [S(f"d_sp{i}") for i in range(ndma)], "qsp")
        self.q_pool = DmaQ(self.pool, [S(f"d_pl{i}") for i in range(ndma)], "qpl")
        self.cc_sem = S("s_cc")
        self.cc_count = 0
        self.scope = stack

    def sb(self, name, shape, dt):
        return TT(self.scope.enter_context(self.nc.sbuf_tensor(name, shape, dt)), name)

    def ps(self, name, shape, dt):
        return TT(self.scope.enter_context(self.nc.psum_tensor(name, shape, dt)), name)

    def barrier(self):
        engs = [self.pe, self.dve, self.act, self.pool, self.sp]
        toks = [Tok(e.name, e.sem, e.count, e) for e in engs if e.count > 0]
        for q in (self.q_sp, self.q_pool):
            toks += [t for t in q.toks if t is not None]
        for e in engs:
            for t in toks:
                if t.eng is not e:
                    e._wait(t)

    def collective(self, kind, ins_ap, outs_ap, groups, reads=(), writes=()):
        e = self.pool
        e.deps(reads, writes)
        ins = e.h.collective_compute(kind, ALU.bypass, replica_groups=groups,
                                     ins=[ins_ap], outs=[outs_ap])
        ins.then_inc(self.cc_sem, 1)
        self.cc_count += 1
        tok = Tok("cc", self.cc_sem, self.cc_count, None)
        for b in reads:
            b.r[tok.key] = tok
        for b in writes:
            b.w = tok
            b.r = {}
        return tok

    def finish(self, bufs):
        for b in bufs:
            self.pool._wait(b.w)

    def mm(self, out, lhsT, rhs, start, stop, R, W, sig=None):
        nc = self.nc
        if sig is None:
            sig = stop
        return self.pe.op(lambda: nc.tensor.matmul(out, lhsT, rhs, start=start, stop=stop), R, W, sig=sig)

    def tr(self, out, in_, ident, R, W, sig=True):
        nc = self.nc
        return self.pe.op(lambda: nc.tensor.transpose(out, in_, ident), R, W, sig=sig)

    def act_(self, out, in_, func, R, W, bias=None, scale=None, accum_out=None):
        nc = self.nc
        kw = {}
        if bias is not None:
            kw["bias"] = bias
        if scale is not None:
            kw["scale"] = scale
        if accum_out is not None:
            kw["accum_out"] = accum_out
        return self.act.op(lambda: nc.scalar.activation(out=out, in_=in_, func=func, **kw), R, W)

    def tt(self, out, in0, in1, op, R, W, eng=None):
        e = eng or self.dve
        return e.op(lambda: e.h.tensor_tensor(out=out, in0=in0, in1=in1, op=op), R, W)

    def ts(self, out, in0, s1, op0, R, W, s2=None, op1=None, eng=None):
        e = eng or self.dve
        if op1 is None:
            return e.op(lambda: e.h.tensor_scalar(out=out, in0=in0, scalar1=s1, scalar2=None, op0=op0), R, W)
        return e.op(lambda: e.h.tensor_scalar(out=out, in0=in0, scalar1=s1, scalar2=s2, op0=op0, op1=op1), R, W)

    def stt(self, out, in0, scalar, in1, op0, op1, R, W):
        nc = self.nc
        return self.dve.op(lambda: nc.vector.scalar_tensor_tensor(out=out, in0=in0, scalar=scalar, in1=in1,
                                                                  op0=op0, op1=op1), R, W)

    def cp(self, out, in_, R, W, eng=None):
        e = eng or self.dve
        return e.op(lambda: e.h.tensor_copy(out=out, in_=in_), R, W)

    def recip(self, out, in_, R, W):
        nc = self.nc
        return self.dve.op(lambda: nc.vector.reciprocal(out=out, in_=in_), R, W)

    def memset(self, ap, val, W, eng=None):
        e = eng or self.dve
        return e.op(lambda: e.h.memset(ap, val), (), W)


def const_tables():
    i = np.arange(128)
    same = (i[:, None] // 64) == (i[None, :] // 64)
    c32 = np.zeros((128, 7, 128), np.float32)
    c32[:, 0, :] = np.eye(128)
    c32[:, 1, :] = (same & (i[:, None] <= i[None, :]))
    c32[:, 2, :] = same
    c32[:, 3, :] = 1.0
    c32[:, 4, :] = np.where(same & (i[None, :] > i[:, None]), 0.0, NEG)
    c32[:, 5, :] = np.where(same & (i[None, :] >= i[:, None]), 0.0, NEG)
    c32[:, 6, :] = 0.0
    cbf = np.zeros((128, 3, 128), np.float32)
    cbf[:, 0, :] = np.eye(128)
    cbf[:, 1, :] = 1.0
    cbf[:, 2, :] = same
    return c32.reshape(128, 7 * 128), cbf.reshape(128, 384).astype(ml_dtypes.bfloat16)


class Consts:
    def __init__(self, fw, nc, c32_ap, cbf_ap):
        self.c32 = fw.sb("c32_sb", [128, 7, 128], F32)
        self.cbf = fw.sb("cbf_sb", [128, 3, 128], BF16)
        self.eps = fw.sb("c_eps", [128, 1], F32)
        self.lnq = fw.sb("c_lnq", [128, 1], F32)
        self.zero = fw.sb("c_zero", [128, 1], F32)
        fw.q_sp.dma(self.c32.t[:].rearrange("p a b -> p (a b)"), c32_ap, writes=[self.c32.b])
        fw.q_sp.dma(self.cbf.t[:].rearrange("p a b -> p (a b)"), cbf_ap, writes=[self.cbf.b])
        fw.memset(self.eps.t[:], EPS, [self.eps.b])
        fw.memset(self.lnq.t[:], -0.5 * float(np.log(128.0)), [self.lnq.b])
        fw.memset(self.zero.t[:], 0.0, [self.zero.b])
        t = self.c32.t
        self.I32, self.TRI, self.BONES, self.ONES = t[:, 0, :], t[:, 1, :], t[:, 2, :], t[:, 3, :]
        self.MSU, self.MU = t[:, 4, :], t[:, 5, :]
        self.Ibf, self.ONESbf, self.BONESbf = self.cbf.t[:, 0, :], self.cbf.t[:, 1, :], self.cbf.t[:, 2, :]


def _rr(lanes, extra=None, extra_turns=2):
    lanes = list(lanes)
    while lanes:
        for g in list(lanes):
            try:
                next(g)
            except StopIteration:
                lanes.remove(g)
        if extra is not None and extra[0] is not None:
            for _ in range(extra_turns):
                try:
                    next(extra[0])
                except StopIteration:
                    extra[0] = None
                    break


H_LANES = [4]
B2_STOP = [9]


def phase_H(fw, nc, C, tag, hT_ch, T, w_in, convT, gT, gate_c, ogT_ch, b_hT, b_ogT, on_chunk=None):
    NSC = T // 512
    WC = 1540
    NPAIR = NSC * 8
    W = fw.sb(tag + "_W", [128, 8, WC], BF16)
    cv = fw.sb(tag + "_conv", [128, 8, 4], F32)
    g_sb = fw.sb(tag + "_g", [128, 8], F32)
    gc = fw.sb(tag + "_gc", [128, 2, 8], F32)
    negA = fw.sb(tag + "_negA", [128, 8], F32)
    fw.q_sp.dma(cv.t[:], convT, writes=[cv.b])
    fw.q_sp.dma(g_sb.t[:], gT, writes=[g_sb.b])
    fw.q_sp.dma(gc.t[:], gate_c, writes=[gc.b])
    outer = fw.scope
    with ExitStack() as wsc:
        fw.scope = wsc
        wst = [fw.sb(f"{tag}_wst{i}", [128, WC], F32) for i in range(2)]
        for c in range(8):
            s = wst[c % 2]
            fw.q_sp.dma(s.t[:], w_in[c * 128:(c + 1) * 128, :], writes=[s.b])
            fw.act_(W.t[:, c, :], s.t[:], AF.Copy, [s.b, g_sb.b], [W.b], scale=g_sb.t[:, c:c + 1])
        fw.barrier()
    fw.scope = outer
    hT = [fw.sb(f"{tag}_hT{i}", [128, 8, 512], BF16) for i in range(2)]
    hist = fw.sb(tag + "_hist", [128, 8, 3], F32)
    pre = [fw.sb(f"{tag}_pre{i}", [128, 515], F32) for i in range(2)]
    cvt = [fw.sb(f"{tag}_cvt{i}", [128, 512], F32) for i in range(2)]
    sl = [fw.sb(f"{tag}_sl{g}", [128, 512], BF16) for g in range(4)]
    sqb = [fw.sb(f"{tag}_sqb{i}", [128, 512], BF16) for i in range(2)]
    lnt = [fw.sb(f"{tag}_lnt{i}", [128, 512], F32) for i in range(2)]
    qkn = [[fw.sb(f"{tag}_qkn{p}_{g}", [128, 512], BF16) for g in range(4)] for p in range(2)]
    vT = [[fw.sb(f"{tag}_vT{p}_{g}", [128, 512], BF16) for g in range(4)] for p in range(2)]
    zs = [[fw.sb(f"{tag}_zs{p}_{i}", [128, 512], BF16) for i in range(4)] for p in range(2)]
    GN = "eb beta nlnb ad gtm Gs eG bg kdS tmp8 kdS0 kdS1".split()
    GT = [{n: fw.sb(f"{tag}_{n}{p}", [128, 4, 2], F32) for n in GN} for p in range(2)]
    S32 = [fw.sb(f"{tag}_S32_{h}", [128, 256], F32) for h in range(2)]
    Sbf = [[fw.sb(f"{tag}_Sbf_{h}_{j}", [128, 256], BF16) for j in range(2)] for h in range(2)]
    ogTs = [fw.sb(f"{tag}_ogT{i}", [128, 4, 512], BF16) for i in range(2)]
    NP = 3
    PT = lambda n, shp, dt: [fw.sb(f"{tag}_{n}{i}", shp, dt) for i in range(NP)]
    Rt, T2, E2 = PT("Rt", [128, 256], F32), PT("T2", [128, 256], F32), PT("E2", [128, 256], F32)
    EGB, NA = PT("EGB", [128, 128], F32), PT("NA", [128, 256], BF16)
    XX = [[fw.sb(f"{tag}_XX{i}_{j}", [128, 384], BF16) for j in range(2)] for i in range(NP)]
    PP = PT("TT", [128, 128], BF16)
    RHS, kd, uw = PT("RHS", [128, 384], BF16), PT("kd", [128, 256], BF16), PT("uw", [128, 384], BF16)
    WkT, qd, QeT = PT("WkT", [128, 256], BF16), PT("qd", [128, 128], F32), PT("QeT", [128, 4, 64], BF16)
    ss, og = PT("ss", [128, 1], F32), PT("og", [128, 256], BF16)
    junk = fw.sb(tag + "_junk", [128, 256], F32)
    big = [fw.ps(f"{tag}_big0", [128, 512], F32)]
    otb = fw.ps(tag + "_otb", [128, 1024], BF16)
    gbkq = fw.ps(tag + "_gbkq", [128, 512], F32)
    neu = fw.ps(tag + "_neu", [128, 512], F32)
    solq = fw.ps(tag + "_solq", [128, 512], F32)
    tpo = fw.ps(tag + "_tpo", [128, 1024], BF16)
    poB = fw.ps(tag + "_poB", [128, 512], F32)
    pSB = fw.ps(tag + "_pSB", [128, 512], F32)
    xb = big[0]
    gp = xb.t[:, 0:16].rearrange("p (a b) -> p a b", b=4)
    gg = xb.t[:, 16:32].rearrange("p (a b) -> p a b", b=4)

    fw.act_(negA.t[:], gc.t[:, 1, :], AF.Exp, [gc.b], [negA.b])
    fw.ts(negA.t[:], negA.t[:], -1.0, ALU.mult, [negA.b], [negA.b])
    fw.memset(hist.t[:], 0.0, [hist.b])
    for i in range(NP):
        fw.memset(QeT[i].t[:], 0.0, [QeT[i].b])
    for h in range(2):
        fw.memset(S32[h].t[:], 0.0, [S32[h].b])
        fw.memset(Sbf[h][0].t[:], 0.0, [Sbf[h][0].b])
    negA3 = negA.t[:].rearrange("p (a b) -> p a b", b=2)
    dtb3 = gc.t[:, 0, :].rearrange("p (a b) -> p a b", b=2)

    def load_hT(sc):
        dst = hT[sc % 2]
        fw.q_sp.dma(dst.t[:], hT_ch[sc].rearrange("(c p) t -> p c t", p=128), reads=[b_hT[sc]], writes=[dst.b])

    def lane_P(sc):
        p = sc % 2
        G = GT[p]
        if sc + 1 < NSC:
            load_hT(sc + 1)
        h_ = hT[p]
        for g in range(8):
            pb = big[0]
            for c in range(8):
                fw.mm(pb.t[:], W.t[:, c, g * 128:(g + 1) * 128], h_.t[:, c, :], c == 0, c == 7, [W.b, h_.b], [pb.b])
                if c % 4 == 3:
                    yield
            P_ = pre[g % 2]
            fw.cp(P_.t[:, 0:3], hist.t[:, g, :], [hist.b], [P_.b], eng=fw.pool)
            fw.act_(P_.t[:, 3:515], pb.t[:], AF.Copy, [pb.b], [P_.b]); yield
            fw.cp(hist.t[:, g, :], P_.t[:, 512:515], [P_.b], [hist.b], eng=fw.pool)
            cb = cvt[g % 2]
            fw.ts(cb.t[:], P_.t[:, 0:512], cv.t[:, g, 0:1], ALU.mult, [P_.b, cv.b], [cb.b]); yield
            for j in range(1, 4):
                fw.stt(cb.t[:], P_.t[:, j:j + 512], cv.t[:, g, j:j + 1], cb.t[:], ALU.mult, ALU.add,
                       [P_.b, cv.b, cb.b], [cb.b]); yield
            dst = sl[g] if g < 4 else vT[p][g - 4]
            fw.act_(dst.t[:], cb.t[:], AF.Silu, [cb.b], [dst.b]); yield
        for t in range(4):
            pb = big[0]
            for c in range(8):
                fw.mm(pb.t[:], h_.t[:, c, t * 128:(t + 1) * 128], W.t[:, c, 1024:1536], c == 0, c == 7, [W.b, h_.b], [pb.b])
                if c % 4 == 3:
                    yield
            fw.act_(zs[p][t].t[:], pb.t[:], AF.Silu, [pb.b], [zs[p][t].b]); yield
        for g in range(4):
            sq, ln_, pb = sqb[g % 2], lnt[g % 2], big[0]
            fw.act_(sq.t[:], sl[g].t[:], AF.Square, [sl[g].b], [sq.b]); yield
            fw.mm(pb.t[:], C.ONESbf, sq.t[:], True, True, [C.cbf.b, sq.b], [pb.b]); yield
            fw.act_(ln_.t[:], pb.t[:], AF.Ln, [pb.b, C.eps.b], [ln_.b], bias=C.eps.t[:]); yield
            if g < 2:
                fw.act_(ln_.t[:], ln_.t[:], AF.Exp, [ln_.b, C.lnq.b], [ln_.b], bias=C.lnq.t[:], scale=-0.5)
            else:
                fw.act_(ln_.t[:], ln_.t[:], AF.Exp, [ln_.b], [ln_.b], scale=-0.5)
            yield
            fw.tt(qkn[p][g].t[:], sl[g].t[:], ln_.t[:], ALU.mult, [sl[g].b, ln_.b], [qkn[p][g].b]); yield
        for t in range(4):
            for c in range(8):
                fw.mm(gp[:, t, :], h_.t[:, c, t * 128:(t + 1) * 128], W.t[:, c, 1536:1540], c == 0, c == 7, [W.b, h_.b], [xb.b])
            yield
        eb, beta, nlnb, ad, gtm, Gs = G["eb"], G["beta"], G["nlnb"], G["ad"], G["gtm"], G["Gs"]
        eG, bg, kdS, tmp8, kdS0, kdS1 = G["eG"], G["bg"], G["kdS"], G["tmp8"], G["kdS0"], G["kdS1"]
        fw.act_(eb.t[:], gp[:, :, 0:2], AF.Exp, [xb.b], [eb.b], scale=-1.0); yield
        fw.ts(eb.t[:], eb.t[:], 1.0, ALU.add, [eb.b], [eb.b]); yield
        fw.recip(beta.t[:], eb.t[:], [eb.b], [beta.b]); yield
        fw.act_(nlnb.t[:], eb.t[:], AF.Ln, [eb.b], [nlnb.b]); yield
        fw.ts(nlnb.t[:], nlnb.t[:], -1.0, ALU.mult, [nlnb.b], [nlnb.b]); yield
        fw.tt(ad.t[:], gp[:, :, 2:4], dtb3, ALU.add, [xb.b, gc.b], [ad.b]); yield
        fw.act_(ad.t[:], ad.t[:], AF.Exp, [ad.b], [ad.b]); yield
        fw.ts(ad.t[:], ad.t[:], 1.0, ALU.add, [ad.b], [ad.b]); yield
        fw.act_(ad.t[:], ad.t[:], AF.Ln, [ad.b], [ad.b]); yield
        fw.tt(gtm.t[:], ad.t[:], negA3, ALU.mult, [ad.b, negA.b], [gtm.b]); yield
        for t in range(4):
            fw.mm(gg[:, t, 0:2], C.TRI, gtm.t[:, t, :], True, True, [C.c32.b, gtm.b], [xb.b])
            fw.mm(gg[:, t, 2:4], C.BONES, gtm.t[:, t, :], True, True, [C.c32.b, gtm.b], [xb.b])
        yield
        fw.cp(Gs.t[:], gg[:, :, 0:2], [xb.b], [Gs.b]); yield
        fw.act_(eG.t[:], Gs.t[:], AF.Exp, [Gs.b], [eG.b]); yield
        fw.tt(bg.t[:], eG.t[:], beta.t[:], ALU.mult, [eG.b, beta.b], [bg.b]); yield
        fw.tt(tmp8.t[:], gg[:, :, 2:4], Gs.t[:], ALU.subtract, [xb.b, Gs.b], [tmp8.b]); yield
        fw.act_(kdS.t[:], tmp8.t[:], AF.Exp, [tmp8.b], [kdS.b]); yield
        fw.ts(kdS0.t[:], kdS.t[:], C.BONES[:, 0:1], ALU.mult, [kdS.b, C.c32.b], [kdS0.b]); yield
        fw.ts(kdS1.t[:], kdS.t[:], C.BONES[:, 127:128], ALU.mult, [kdS.b, C.c32.b], [kdS1.b]); yield

    def ctx(g):
        sc, r = divmod(g, 8)
        t, h = divmod(r, 2)
        return sc, sc % 2, t, h, g % NP, slice(t * 128, (t + 1) * 128)

    def lane_F(g):
        sc, p, t, h, i, tk = ctx(g)
        G = GT[p]
        qT, kT = qkn[p][h].t[:, tk], qkn[p][2 + h].t[:, tk]
        qb, kb = qkn[p][h].b, qkn[p][2 + h].b
        gtm, nlnb, Gs = G["gtm"], G["nlnb"], G["Gs"]
        fw.act_(Rt[i].t[:, 128:256], C.TRI, AF.Copy, [C.c32.b, gtm.b], [Rt[i].b], scale=gtm.t[:, t, h:h + 1]); yield
        fw.stt(Rt[i].t[:, 0:128], C.I32, nlnb.t[:, t, h:h + 1], Rt[i].t[:, 128:256], ALU.mult, ALU.add,
               [C.c32.b, nlnb.b, Rt[i].b], [Rt[i].b]); yield
        fw.mm(gbkq.t[:, 0:256], C.ONES, Rt[i].t[:], True, True, [C.c32.b, Rt[i].b], [gbkq.b]); yield
        fw.mm(gbkq.t[:, 256:384], kT, kT, True, True, [kb], [gbkq.b]); yield
        fw.mm(gbkq.t[:, 384:512], kT, qT, True, True, [kb, qb], [gbkq.b]); yield
        fw.stt(T2[i].t[:, 0:128], gbkq.t[:, 0:128], Gs.t[:, t, h:h + 1], C.MSU, ALU.subtract, ALU.add,
               [gbkq.b, Gs.b, C.c32.b], [T2[i].b]); yield
        fw.stt(T2[i].t[:, 128:256], gbkq.t[:, 128:256], Gs.t[:, t, h:h + 1], C.MU, ALU.subtract, ALU.add,
               [gbkq.b, Gs.b, C.c32.b], [T2[i].b]); yield
        fw.act_(E2[i].t[:], T2[i].t[:], AF.Exp, [T2[i].b], [E2[i].b]); yield
        fw.act_(EGB[i].t[:], gbkq.t[:, 128:256], AF.Exp, [gbkq.b], [EGB[i].b]); yield
        fw.tt(NA[i].t[:], gbkq.t[:, 256:512], E2[i].t[:], ALU.mult, [gbkq.b, E2[i].b], [NA[i].b]); yield
        X0, X1 = XX[i]
        fw.mm(gbkq.t[:, 256:384], NA[i].t[:, 0:128], C.Ibf, True, True, [NA[i].b, C.cbf.b], [gbkq.b]); yield
        fw.cp(X0.t[:, 0:128], NA[i].t[:, 0:128], [NA[i].b], [X0.b], eng=fw.pool); yield
        fw.act_(X0.t[:, 256:384], gbkq.t[:, 256:384], AF.Copy, [gbkq.b], [X0.b]); yield
        fw.tt(X1.t[:, 128:256], C.Ibf, NA[i].t[:, 0:128], ALU.subtract, [C.cbf.b, NA[i].b], [X1.b]); yield

    def lane_N(g):
        i = g % NP
        X0, X1 = XX[i]
        fw.mm(neu.t[:, 0:128], X0.t[:, 256:384], X0.t[:, 0:128], True, True, [X0.b], [neu.b], sig=False)
        fw.mm(neu.t[:, 256:384], X0.t[:, 0:128], X0.t[:, 256:384], True, True, [X0.b], [neu.b]); yield
        fw.act_(X1.t[:, 0:128], neu.t[:, 0:128], AF.Copy, [neu.b], [X1.b]); yield
        fw.cp(X1.t[:, 256:384], neu.t[:, 256:384], [neu.b], [X1.b]); yield
        cur, nxt = X1, X0
        for k in range(1, 5):
            if k < 4:
                fw.mm(neu.t[:, 0:256], cur.t[:, 256:384], cur.t[:, 0:256], True, True, [cur.b], [neu.b], sig=False)
            else:
                fw.mm(neu.t[:, 128:256], cur.t[:, 256:384], cur.t[:, 128:256], True, True, [cur.b], [neu.b], sig=False)
            fw.mm(neu.t[:, 256:384], cur.t[:, 0:128], cur.t[:, 256:384], True, True, [cur.b], [neu.b]); yield
            if k < 4:
                fw.act_(nxt.t[:, 0:128], neu.t[:, 0:128], AF.Copy, [neu.b], [nxt.b]); yield
            fw.tt(nxt.t[:, 128:256], neu.t[:, 128:256], cur.t[:, 128:256], ALU.add, [neu.b, cur.b], [nxt.b]); yield
            fw.cp(nxt.t[:, 256:384], neu.t[:, 256:384], [neu.b], [nxt.b]); yield
            cur, nxt = nxt, cur
        fw.mm(neu.t[:, 128:256], cur.t[:, 256:384], cur.t[:, 128:256], True, True, [cur.b], [neu.b]); yield
        fw.tt(PP[i].t[:], neu.t[:, 128:256], cur.t[:, 128:256], ALU.add, [neu.b, cur.b], [PP[i].b]); yield

    def lane_B1(g):
        sc, p, t, h, i, tk = ctx(g)
        G = GT[p]
        qT, kT = qkn[p][h].t[:, tk], qkn[p][2 + h].t[:, tk]
        qb, kb = qkn[p][h].b, qkn[p][2 + h].b
        v0, v1 = vT[p][2 * h], vT[p][2 * h + 1]
        TTt = PP[i]
        fw.tr(tpo.t[:, 0:128], kT, C.Ibf, [kb, C.cbf.b], [tpo.b], sig=False)
        fw.tr(tpo.t[:, 128:256], v0.t[:, tk], C.Ibf, [v0.b, C.cbf.b], [tpo.b], sig=False)
        fw.tr(tpo.t[:, 256:384], v1.t[:, tk], C.Ibf, [v1.b, C.cbf.b], [tpo.b]); yield
        fw.act_(RHS[i].t[:, 0:256], tpo.t[:, 128:384], AF.Copy, [tpo.b, G["beta"].b], [RHS[i].b],
                scale=G["beta"].t[:, t, h:h + 1]); yield
        fw.ts(RHS[i].t[:, 256:384], tpo.t[:, 0:128], G["bg"].t[:, t, h:h + 1], ALU.mult, [tpo.b, G["bg"].b], [RHS[i].b]); yield
        fw.ts(kd[i].t[:, 0:128], tpo.t[:, 0:128], G["kdS0"].t[:, t, h:h + 1], ALU.mult, [tpo.b, G["kdS0"].b], [kd[i].b]); yield
        fw.ts(kd[i].t[:, 128:256], tpo.t[:, 0:128], G["kdS1"].t[:, t, h:h + 1], ALU.mult, [tpo.b, G["kdS1"].b], [kd[i].b]); yield
        fw.mm(solq.t[:, 0:384], TTt.t[:], RHS[i].t[:], True, True, [TTt.b, RHS[i].b], [solq.b]); yield
        fw.act_(uw[i].t[:, 0:256], solq.t[:, 0:256], AF.Copy, [solq.b], [uw[i].b]); yield
        fw.ts(uw[i].t[:, 256:384], solq.t[:, 256:384], -1.0, ALU.mult, [solq.b], [uw[i].b]); yield
        fw.mm(solq.t[:, 0:256], uw[i].t[:, 256:384], kd[i].t[:], True, True, [uw[i].b, kd[i].b], [solq.b]); yield
        fw.act_(WkT[i].t[:], solq.t[:, 0:256], AF.Copy, [solq.b], [WkT[i].b]); yield
        fw.mm(solq.t[:, 384:512], uw[i].t[:, 256:384], NA[i].t[:, 128:256], True, True, [uw[i].b, NA[i].b], [solq.b]); yield
        fw.tt(qd[i].t[:], qT, EGB[i].t[:], ALU.mult, [qb, EGB[i].b], [qd[i].b]); yield
        fw.tt(QeT[i].t[:, 0:4:3, :], solq.t[:, 384:512].rearrange("p (a b) -> p a b", b=64),
              qd[i].t[:].rearrange("p (a b) -> p a b", b=64), ALU.add, [solq.b, qd[i].b], [QeT[i].b]); yield

    def lane_B2(g):
        sc, p, t, h, i, tk = ctx(g)
        osb = ogTs[p]
        S0, S1 = Sbf[h]
        po, pS = poB.t[:, 0:256], pSB.t[:, 0:256]
        Q2 = QeT[i].t[:].rearrange("p a b -> p (a b)")
        u_ = uw[i].t[:, 0:256]

        def s_update(j, Sin, Sout):
            fw.mm(pS, kd[i].t[:, j * 128:(j + 1) * 128], u_, True, False, [kd[i].b, uw[i].b], [pSB.b], sig=False)
            fw.mm(pS, WkT[i].t[:, j * 128:(j + 1) * 128], Sin.t[:], False, True, [WkT[i].b, Sin.b], [pSB.b])
            yield
            glc = EGB[i].t[:, 64 * j + 63:64 * j + 64]
            fw.stt(Sout.t[:], S32[h].t[:], glc, pS, ALU.mult, ALU.add, [S32[h].b, EGB[i].b, pSB.b], [Sout.b]); yield
            fw.stt(S32[h].t[:], S32[h].t[:], glc, pS, ALU.mult, ALU.add, [S32[h].b, EGB[i].b, pSB.b], [S32[h].b]); yield

        yield from s_update(0, S0, S1)
        if B2_STOP[0] <= 1:
            return
        fw.mm(po, NA[i].t[:, 128:256], u_, True, False, [NA[i].b, uw[i].b], [poB.b], sig=False)
        fw.mm(po, Q2[:, 0:128], S0.t[:], False, False, [QeT[i].b, S0.b], [poB.b], sig=False)
        fw.mm(po, Q2[:, 128:256], S1.t[:], False, True, [QeT[i].b, S1.b], [poB.b]); yield
        if B2_STOP[0] <= 2:
            return
        yield from s_update(1, S1, S0)
        if B2_STOP[0] <= 3:
            return

    def lane_C(g):
        sc, p, t, h, i, tk = ctx(g)
        osb = ogTs[p]
        po = poB.t[:, 0:256]
        fw.act_(junk.t[:], po, AF.Square, [poB.b], [junk.b, ss[i].b], accum_out=ss[i].t[:]); yield
        fw.act_(ss[i].t[:], ss[i].t[:], AF.Ln, [ss[i].b, C.eps.b], [ss[i].b], bias=C.eps.t[:], scale=1.0 / 256); yield
        fw.act_(ss[i].t[:], ss[i].t[:], AF.Exp, [ss[i].b], [ss[i].b], scale=-0.5); yield
        hc = slice(h * 256, (h + 1) * 256)
        fw.stt(og[i].t[:], po, ss[i].t[:], zs[p][t].t[:, hc], ALU.mult, ALU.mult, [poB.b, ss[i].b, zs[p][t].b], [og[i].b]); yield
        if B2_STOP[0] <= 4:
            return
        for v in range(2):
            oc = h * 2 + v
            fw.tr(otb.t[:, oc * 128:(oc + 1) * 128], og[i].t[:, v * 128:(v + 1) * 128], C.Ibf,
                  [og[i].b, C.cbf.b], [otb.b], sig=(v == 1))
        yield
        if B2_STOP[0] <= 5:
            return
        for v in range(2):
            oc = h * 2 + v
            fw.act_(osb.t[:, oc, tk], otb.t[:, oc * 128:(oc + 1) * 128], AF.Copy, [otb.b], [osb.b]); yield
        if g % 8 == 7:
            fw.q_pool.dma(ogT_ch[sc].rearrange("(a p) t -> p a t", p=128), osb.t[:], reads=[osb.b], writes=[b_ogT[sc]])
            if on_chunk is not None:
                on_chunk(sc)


    def lane_B(g):
        yield from lane_B1(g)
        if H_LANES[0] >= 4:
            yield from lane_B2(g)

    load_hT(0)
    for _ in lane_P(0):
        pass
    cur_P = [None]
    for step in range(NPAIR + 3):
        if step < NPAIR and step % 8 == 0 and cur_P[0] is not None:
            for _ in cur_P[0]:
                pass
            cur_P[0] = None
        if step % 8 == 3 and step // 8 + 1 < NSC:
            cur_P[0] = lane_P(step // 8 + 1)
        lanes = []
        if step < NPAIR:
            lanes.append(lane_F(step))
        if 0 <= step - 1 < NPAIR:
            lanes.append(lane_N(step - 1))
        if 0 <= step - 2 < NPAIR:
            lanes.append(lane_B(step - 2))
        if 0 <= step - 3 < NPAIR:
            lanes.append(lane_C(step - 3))
        _rr(lanes, cur_P, 2)
    if cur_P[0] is not None:
        for _ in cur_P[0]:
            pass
    if H_LANES[0] < 4 or B2_STOP[0] < 9:
        for sc in range(NSC):
            fw.q_pool.dma(ogT_ch[sc].rearrange("(a p) t -> p a t", p=128), ogTs[sc % 2].t[:], reads=[ogTs[sc % 2].b], writes=[b_ogT[sc]])


def phase_ON(fw, nc, C, tag, NTOK, x_in, x_out, og_src, w_out, gainT, hT_out, b_xin, b_xout, b_og, b_hTout):
    NT = NTOK // 128
    do_proj = og_src is not None
    do_norm = hT_out is not None
    xt = [fw.sb(f"{tag}_xt{i}", [128, 1024], F32) for i in range(2)]
    big = [fw.ps(f"{tag}_big{i}", [128, 512], F32) for i in range(4)]
    if do_proj:
        Wo = fw.sb(f"{tag}_Wo", [128, 16, 1024], BF16)
        wst = [fw.sb(f"{tag}_wst{i}", [128, 1024], F32) for i in range(2)]
        gsb = fw.sb(f"{tag}_gain", [128, 2], F32)
        ogt = [fw.sb(f"{tag}_ogt{i}", [128, 16, 512], BF16) for i in range(2)]
        fw.q_sp.dma(gsb.t[:], gainT, writes=[gsb.b])
        for r in range(16):
            s = wst[r % 2]
            fw.q_sp.dma(s.t[:], w_out[r * 128:(r + 1) * 128, :], writes=[s.b])
            fw.act_(Wo.t[:, r, :], s.t[:], AF.Copy, [s.b, gsb.b], [Wo.b], scale=gsb.t[:, (r % 2):(r % 2) + 1])
    if do_norm:
        sq = fw.sb(f"{tag}_sq", [128, 1024], F32)
        ss = [fw.sb(f"{tag}_ss{i}", [128, 1], F32) for i in range(2)]
        xb = [fw.sb(f"{tag}_xb{i}", [128, 1024], F32) for i in range(2)]
        hT = [fw.sb(f"{tag}_hT{i}", [128, 8, 512], BF16) for i in range(2)]

    def load_og(sb_):
        dst = ogt[sb_ % 2]
        fw.q_sp.dma(dst.t[:].rearrange("p (a b) t -> p a b t", a=4),
                    og_src.rearrange("a (b p) t -> p a b t", p=128)[:, :, :, sb_ * 512:(sb_ + 1) * 512],
                    reads=[b_og], writes=[dst.b])

    if do_proj:
        load_og(0)
    for t in range(NT):
        sb_, tt_ = t // 4, t % 4
        tk = slice(tt_ * 128, (tt_ + 1) * 128)
        if do_proj and tt_ == 0 and (sb_ + 1) * 4 < NT:
            load_og(sb_ + 1)
        X = xt[t % 2]
        fw.q_sp.dma(X.t[:], x_in[t * 128:(t + 1) * 128, :], reads=[b_xin], writes=[X.b])
        if do_proj:
            o_ = ogt[sb_ % 2]
            for half in range(2):
                pb = big[half]
                for r in range(16):
                    fw.mm(pb.t[:], o_.t[:, r, tk], Wo.t[:, r, half * 512:(half + 1) * 512], r == 0, r == 15,
                          [o_.b, Wo.b], [pb.b])
                fw.tt(X.t[:, half * 512:(half + 1) * 512], pb.t[:], X.t[:, half * 512:(half + 1) * 512], ALU.add,
                      [pb.b, X.b], [X.b])
        if x_out is not None:
            fw.q_pool.dma(x_out[t * 128:(t + 1) * 128, :], X.t[:], reads=[X.b], writes=[b_xout])
        if do_norm:
            s_ = ss[t % 2]
            B_ = xb[t % 2]
            H_ = hT[sb_ % 2]
            fw.act_(sq.t[:], X.t[:], AF.Square, [X.b], [sq.b, s_.b], accum_out=s_.t[:])
            fw.ts(s_.t[:], s_.t[:], 1.0 / D, ALU.mult, [s_.b], [s_.b], s2=EPS, op1=ALU.add)
            fw.act_(s_.t[:], s_.t[:], AF.Sqrt, [s_.b], [s_.b])
            fw.recip(s_.t[:], s_.t[:], [s_.b], [s_.b])
            fw.ts(B_.t[:], X.t[:], s_.t[:], ALU.mult, [X.b, s_.b], [B_.b])
            for g4 in range(2):
                pb = big[2 + g4]
                for cc in range(4):
                    c = g4 * 4 + cc
                    fw.tr(pb.t[:, cc * 128:(cc + 1) * 128], B_.t[:, c * 128:(c + 1) * 128], C.I32,
                          [B_.b, C.c32.b], [pb.b], sig=(cc == 3))
                for cc in range(4):
                    c = g4 * 4 + cc
                    if cc % 2 == 0:
                        fw.act_(H_.t[:, c, tk], pb.t[:, cc * 128:(cc + 1) * 128], AF.Copy, [pb.b], [H_.b])
                    else:
                        fw.cp(H_.t[:, c, tk], pb.t[:, cc * 128:(cc + 1) * 128], [pb.b], [H_.b])
            if tt_ == 3 or t == NT - 1:
                n_ = (tt_ + 1) * 128
                fw.q_pool.dma(hT_out.rearrange("(c p) t -> p c t", p=128)[:, :, sb_ * 512:sb_ * 512 + n_],
                              H_.t[:, :, 0:n_], reads=[H_.b], writes=[b_hTout])


def build_ebt(fw, nc, C, tag, relb, onehot, scratch, ebt_dram, b_ebt):
    rb = fw.sb(f"{tag}_rb", [32, 8], F32)
    oh = fw.sb(f"{tag}_oh", [32, 128], F32)
    EBT = fw.sb(f"{tag}_EBT", [128, 2, 8, 128], BF16)
    rhsE = fw.sb(f"{tag}_rhsE", [32, 1024], F32)
    Fsb = fw.sb(f"{tag}_Fsb", [128, 8, 384], F32)
    EBf = fw.sb(f"{tag}_EBf", [128, 2, 1024], F32)
    big = [fw.ps(f"{tag}_big{i}", [128, 512], F32) for i in range(2)]
    fw.q_sp.dma(rb.t[:], relb, writes=[rb.b])
    fw.q_sp.dma(oh.t[:], onehot, writes=[oh.b])
    for h in range(8):
        fw.ts(rhsE.t[:, h * 128:(h + 1) * 128], oh.t[:], rb.t[:, h:h + 1], ALU.mult, [oh.b, rb.b], [rhsE.b])
    fw.memset(Fsb.t[:], 0.0, [Fsb.b], eng=fw.pool)
    for k2 in range(2):
        fw.mm(big[k2].t[:], C.ONES[0:32, :], rhsE.t[:, k2 * 512:(k2 + 1) * 512], True, True, [C.c32.b, rhsE.b], [big[k2].b])
        for hh in range(4):
            h = k2 * 4 + hh
            fw.act_(Fsb.t[:, h, 128:256], big[k2].t[:, hh * 128:(hh + 1) * 128], AF.Exp, [big[k2].b], [Fsb.b])
    b_scr = Buf("scratch")
    wr = bass.AP(scratch.tensor, 0, [[385, 128], [128 * 385, 8], [1, 384]])
    fw.q_pool.dma(wr, Fsb.t[:], reads=[Fsb.b], writes=[b_scr])
    for kb in range(2):
        rd = bass.AP(scratch.tensor, 256 - 128 * kb, [[384, 128], [128 * 385, 8], [1, 128]])
        fw.q_pool.dma(EBf.t[:, kb, :].rearrange("p (h q) -> p h q", q=128), rd, reads=[b_scr], writes=[EBf.b])
    fw.cp(EBT.t[:].rearrange("p a h q -> p a (h q)"), EBf.t[:], [EBf.b], [EBT.b])
    fw.q_pool.dma(ebt_dram, EBT.t[:].rearrange("p a h q -> p (a h q)"), reads=[EBT.b], writes=[b_ebt])


def phase_HB(fw, nc, C, tag, T, hT_ch, hTkv_ch, w_qz, w_kv, bgT, kvgT, hp_c, relb, onehot, scratch,
             ogT_ch, b_hT, b_hTkv, b_ogT, on_chunk=None, kv_dram=None, kv_load=False):
    NSB = T // 512
    NBLK = T // 128
    Wqz = fw.sb(f"{tag}_Wqz", [128, 8, 1024], BF16)
    Wkv = fw.sb(f"{tag}_Wkv", [128, 8, 192], BF16)
    g1 = fw.sb(f"{tag}_g1", [128, 8], F32)
    g2 = fw.sb(f"{tag}_g2", [128, 8], F32)
    hp = fw.sb(f"{tag}_hp", [128, 8], F32)
    qg8 = fw.sb(f"{tag}_qg8", [128, 1], F32)
    qg8A = fw.sb(f"{tag}_qg8A", [128, 1], F32)
    qg8B = fw.sb(f"{tag}_qg8B", [128, 1], F32)
    esink = fw.sb(f"{tag}_esink", [128, 4], F32)
    EBT = fw.sb(f"{tag}_EBT", [128, 2, 8, 128], BF16)
    big = [fw.ps(f"{tag}_big{i}", [128, 512], F32) for i in range(2)]
    SA = [fw.ps(f"{tag}_SA{i}", [128, 512], F32) for i in range(2)]
    SBk = [fw.ps(f"{tag}_SB{i}", [128, 512], F32) for i in range(2)]
    oT = fw.ps(f"{tag}_oT", [128, 512], F32)
    den = fw.ps(f"{tag}_den", [128, 512], F32)

    fw.q_sp.dma(g1.t[:], bgT, writes=[g1.b])
    fw.q_sp.dma(g2.t[:], kvgT, writes=[g2.b])
    fw.q_sp.dma(hp.t[:], hp_c, writes=[hp.b])
    outer_w = fw.scope
    with ExitStack() as wsc:
        fw.scope = wsc
        wst = [fw.sb(f"{tag}_wst{i}", [128, 1024], F32) for i in range(2)]
        wst2 = [fw.sb(f"{tag}_wstk{i}", [128, 192], F32) for i in range(2)]
        for c in range(8):
            s = wst[c % 2]
            fw.q_sp.dma(s.t[:], w_qz[c * 128:(c + 1) * 128, :], writes=[s.b])
            fw.act_(Wqz.t[:, c, :], s.t[:], AF.Copy, [s.b, g1.b], [Wqz.b], scale=g1.t[:, c:c + 1])
            s2 = wst2[c % 2]
            fw.q_sp.dma(s2.t[:], w_kv[c * 128:(c + 1) * 128, :], writes=[s2.b])
            fw.act_(Wkv.t[:, c, :], s2.t[:], AF.Copy, [s2.b, g2.b], [Wkv.b], scale=g2.t[:, c:c + 1])
        fw.barrier()
    fw.scope = outer_w
    fw.ts(qg8.t[:], hp.t[:, 0:1], 0.125, ALU.mult, [hp.b], [qg8.b])
    fw.ts(qg8A.t[:], qg8.t[:], C.BONES[:, 0:1], ALU.mult, [qg8.b, C.c32.b], [qg8A.b])
    fw.ts(qg8B.t[:], qg8.t[:], C.BONES[:, 127:128], ALU.mult, [qg8.b, C.c32.b], [qg8B.b])
    fw.act_(esink.t[:], hp.t[:, 2:6], AF.Exp, [hp.b], [esink.b])
    fw.q_pool.dma(EBT.t[:].rearrange("p a h q -> p (a h q)"), kv_dram["ebt"], reads=[kv_dram["buf_ebt"]], writes=[EBT.b])
    KTAB = fw.sb(f"{tag}_KTAB", [128, T + 128], BF16)
    Vpad = fw.sb(f"{tag}_Vpad", [128, NBLK + 1, 192], BF16)
    OP = fw.sb(f"{tag}_OP", [128, 192], BF16)
    hq = [fw.sb(f"{tag}_hq{i}", [128, 8, 512], BF16) for i in range(2)]
    sqb = [fw.sb(f"{tag}_sqb{i}", [128, 512], BF16) for i in range(2)]
    lnt = [fw.sb(f"{tag}_lnt{i}", [128, 512], F32) for i in range(2)]
    QT = [fw.sb(f"{tag}_QT{j}", [128, 512], BF16) for j in range(4)]
    zsT = [fw.sb(f"{tag}_zsT{j}", [128, 512], BF16) for j in range(4)]
    PA = [fw.sb(f"{tag}_PA{i}", [128, 4, 128], BF16) for i in range(2)]
    PB = [fw.sb(f"{tag}_PB{i}", [128, 4, 128], BF16) for i in range(2)]
    lnd = fw.sb(f"{tag}_lnd", [128, 512], F32)
    rr = fw.sb(f"{tag}_rr", [128, 512], F32)
    rz = fw.sb(f"{tag}_rz", [128, 512], F32)
    osb = [fw.sb(f"{tag}_osb{i}", [128, 4, 512], BF16) for i in range(2)]
    if not kv_load:
        fw.memset(Vpad.t[:], 0.0, [Vpad.b], eng=fw.pool)
        fw.memset(KTAB.t[:, 0:128], 0.0, [KTAB.b], eng=fw.pool)
    fw.memset(OP.t[:], 0.0, [OP.b], eng=fw.pool)
    fw.memset(OP.t[:, 64:128], 1.0, [OP.b], eng=fw.pool)

    def load_h(src, bsrc, dst, sc):
        fw.q_sp.dma(dst.t[:], src[sc].rearrange("(c p) t -> p c t", p=128), reads=[bsrc[sc]], writes=[dst.b])

    def kv_for_sb(sc, h_):
        pk, pq = big[0], big[1]
        for c in range(8):
            fw.mm(pk.t[:], Wkv.t[:, c, 0:128], h_.t[:, c, :], c == 0, c == 7, [Wkv.b, h_.b], [pk.b])
            if c % 4 == 3:
                yield
        sq, ln_ = sqb[0], lnt[0]
        fw.act_(sq.t[:], pk.t[:], AF.Square, [pk.b], [sq.b]); yield
        fw.mm(pq.t[:], C.BONESbf, sq.t[:], True, True, [C.cbf.b, sq.b], [pq.b]); yield
        fw.act_(ln_.t[:], pq.t[:], AF.Ln, [pq.b, C.eps.b], [ln_.b], bias=C.eps.t[:], scale=1.0 / 64); yield
        fw.act_(ln_.t[:], ln_.t[:], AF.Exp, [ln_.b], [ln_.b], scale=-0.5); yield
        fw.stt(KTAB.t[:, 128 + sc * 512:128 + (sc + 1) * 512], pk.t[:], hp.t[:, 1:2], ln_.t[:], ALU.mult, ALU.mult,
               [pk.b, hp.b, ln_.b], [KTAB.b]); yield
        for t in range(4):
            for c in range(8):
                fw.mm(pq.t[:, t * 64:(t + 1) * 64], h_.t[:, c, t * 128:(t + 1) * 128], Wkv.t[:, c, 128:192],
                      c == 0, c == 7, [Wkv.b, h_.b], [pq.b])
            yield
        for t in range(4):
            fw.act_(Vpad.t[:, sc * 4 + t + 1, 64:128], pq.t[:, t * 64:(t + 1) * 64], AF.Copy, [pq.b], [Vpad.b]); yield

    if kv_load:
        fw.q_sp.dma(KTAB.t[:], kv_dram["ktab"], reads=[kv_dram["buf"]], writes=[KTAB.b])
        fw.q_pool.dma(Vpad.t[:].rearrange("p a b -> p (a b)"), kv_dram["vpad"], reads=[kv_dram["buf"]], writes=[Vpad.b])

    QT2 = [QT, [fw.sb(f"{tag}_QTb{j}", [128, 512], BF16) for j in range(4)]]
    zsT2 = [zsT, [fw.sb(f"{tag}_zsTb{j}", [128, 512], BF16) for j in range(4)]]
    PA2 = [PA, [fw.sb(f"{tag}_PAb{i}", [128, 4, 128], BF16) for i in range(2)]]
    PB2 = [PB, [fw.sb(f"{tag}_PBb{i}", [128, 4, 128], BF16) for i in range(2)]]

    def lane_Q(sc):
        p = sc % 2
        if sc + 1 < NSB:
            load_h(hT_ch, b_hT, hq[(sc + 1) % 2], sc + 1)
        h_ = hq[p]
        if not kv_load:
            yield from kv_for_sb(sc, h_)
        pb, p2 = big[0], big[1]
        for j in range(4):
            for c in range(8):
                fw.mm(pb.t[:], Wqz.t[:, c, j * 128:(j + 1) * 128], h_.t[:, c, :], c == 0, c == 7, [Wqz.b, h_.b], [pb.b])
                if c % 4 == 3:
                    yield
            sq, ln_ = sqb[j % 2], lnt[j % 2]
            fw.act_(sq.t[:], pb.t[:], AF.Square, [pb.b], [sq.b]); yield
            fw.mm(p2.t[:], C.BONESbf, sq.t[:], True, True, [C.cbf.b, sq.b], [p2.b]); yield
            fw.act_(ln_.t[:], p2.t[:], AF.Ln, [p2.b, C.eps.b], [ln_.b], bias=C.eps.t[:], scale=1.0 / 64); yield
            fw.act_(ln_.t[:], ln_.t[:], AF.Exp, [ln_.b], [ln_.b], scale=-0.5); yield
            fw.stt(QT2[p][j].t[:], pb.t[:], qg8.t[:], ln_.t[:], ALU.mult, ALU.mult, [pb.b, qg8.b, ln_.b], [QT2[p][j].b]); yield
        for j in range(4):
            for c in range(8):
                fw.mm(pb.t[:], Wqz.t[:, c, 512 + j * 128:512 + (j + 1) * 128], h_.t[:, c, :], c == 0, c == 7,
                      [Wqz.b, h_.b], [pb.b])
                if c % 4 == 3:
                    yield
            fw.act_(zsT2[p][j].t[:], pb.t[:], AF.Silu, [pb.b], [zsT2[p][j].b]); yield

    def lane_S(n):
        sc, tt_ = divmod(n, 4)
        Q_ = QT2[sc % 2]
        tk = slice(tt_ * 128, (tt_ + 1) * 128)
        kbs = [1] if n == 0 else [0, 1]
        for kb in kbs:
            kc = slice((n + kb) * 128, (n + kb + 1) * 128)
            sa, sbk, pa, pb_ = SA[kb], SBk[kb], PA2[n % 2][kb], PB2[n % 2][kb]
            for j in range(4):
                fw.mm(sa.t[:, j * 128:(j + 1) * 128], KTAB.t[0:64, kc], Q_[j].t[0:64, tk], True, True,
                      [KTAB.b, Q_[j].b], [sa.b], sig=(j == 3))
            yield
            for j in range(4):
                fw.mm(sbk.t[:, j * 128:(j + 1) * 128], KTAB.t[64:128, kc], Q_[j].t[64:128, tk], True, True,
                      [KTAB.b, Q_[j].b], [sbk.b], sig=(j == 3))
            yield
            paf = pa.t[:].rearrange("p a b -> p (a b)")
            pbf = pb_.t[:].rearrange("p a b -> p (a b)")
            fw.act_(paf, sa.t[:], AF.Exp, [sa.b], [pa.b]); yield
            fw.act_(pbf, sbk.t[:], AF.Exp, [sbk.b], [pb_.b]); yield
            fw.tt(pa.t[:], pa.t[:], EBT.t[:, kb, 0:8:2, :], ALU.mult, [pa.b, EBT.b], [pa.b]); yield
            fw.tt(pb_.t[:], pb_.t[:], EBT.t[:, kb, 1:8:2, :], ALU.mult, [pb_.b, EBT.b], [pb_.b]); yield

    def lane_O(n):
        sc, tt_ = divmod(n, 4)
        Z_ = zsT2[sc % 2]
        ob = osb[sc % 2]
        tk = slice(tt_ * 128, (tt_ + 1) * 128)
        kbs = [1] if n == 0 else [0, 1]
        first = True
        for kb in kbs:
            paf = PA2[n % 2][kb].t[:].rearrange("p a b -> p (a b)")
            pbf = PB2[n % 2][kb].t[:].rearrange("p a b -> p (a b)")
            fw.mm(oT.t[:], Vpad.t[:, n + kb, 64:192], paf, first, False, [Vpad.b, PA2[n % 2][kb].b], [oT.b], sig=False)
            fw.mm(oT.t[:], Vpad.t[:, n + kb, 0:128], pbf, False, kb == 1, [Vpad.b, PB2[n % 2][kb].b], [oT.b], sig=(kb == 1))
            first = False
        yield
        first = True
        for kb in kbs:
            paf = PA2[n % 2][kb].t[:].rearrange("p a b -> p (a b)")
            pbf = PB2[n % 2][kb].t[:].rearrange("p a b -> p (a b)")
            fw.mm(den.t[:], OP.t[:, 64:192], paf, first, False, [OP.b, PA2[n % 2][kb].b], [den.b], sig=False)
            fw.mm(den.t[:], OP.t[:, 0:128], pbf, False, kb == 1, [OP.b, PB2[n % 2][kb].b], [den.b], sig=(kb == 1))
            first = False
        yield
        for j in range(4):
            fw.act_(lnd.t[:, j * 128:(j + 1) * 128], den.t[:, j * 128:(j + 1) * 128], AF.Ln, [den.b, esink.b], [lnd.b],
                    bias=esink.t[:, j:j + 1]); yield
        fw.act_(rr.t[:], lnd.t[:], AF.Exp, [lnd.b], [rr.b], scale=-1.0); yield
        for j in range(4):
            fw.tt(rz.t[:, j * 128:(j + 1) * 128], rr.t[:, j * 128:(j + 1) * 128], Z_[j].t[:, tk], ALU.mult,
                  [rr.b, Z_[j].b], [rz.b], eng=fw.pool); yield
        fw.tt(ob.t[:, :, tk], oT.t[:].rearrange("p (a b) -> p a b", b=128), rz.t[:].rearrange("p (a b) -> p a b", b=128),
              ALU.mult, [oT.b, rz.b], [ob.b]); yield
        if tt_ == 3:
            fw.q_pool.dma(ogT_ch[sc].rearrange("(a p) t -> p a t", p=128), ob.t[:], reads=[ob.b], writes=[b_ogT[sc]])
            if on_chunk is not None:
                on_chunk(sc)

    load_h(hT_ch, b_hT, hq[0], 0)
    for _ in lane_Q(0):
        pass
    cur_Q = [None]
    for step in range(NBLK + 1):
        if step < NBLK and step % 4 == 0 and cur_Q[0] is not None:
            for _ in cur_Q[0]:
                pass
            cur_Q[0] = None
        if step % 4 == 1 and step // 4 + 1 < NSB:
            cur_Q[0] = lane_Q(step // 4 + 1)
        lanes = []
        if step < NBLK:
            lanes.append(lane_S(step))
        if 0 <= step - 1 < NBLK:
            lanes.append(lane_O(step - 1))
        _rr(lanes, cur_Q, 3)
    if cur_Q[0] is not None:
        for _ in cur_Q[0]:
            pass
    if kv_dram is not None and not kv_load:
        fw.q_pool.dma(kv_dram["ktab"], KTAB.t[:], reads=[KTAB.b], writes=[kv_dram["buf"]])
        fw.q_pool.dma(kv_dram["vpad"], Vpad.t[:].rearrange("p a b -> p (a b)"), reads=[Vpad.b], writes=[kv_dram["buf"]])


def _bucket_onehot():
    n = np.arange(128)
    max_exact = 16
    large = max_exact + (np.log(np.maximum(n, 1) / max_exact) / np.log(128 / max_exact) * (32 - max_exact)).astype(np.int64)
    large = np.minimum(large, 31)
    b = np.where(n < max_exact, n, large)
    oh = np.zeros((32, 128), np.float32)
    oh[b, n] = 1.0
    return oh


def prep_HA(a_w_in, a_conv, a_norm, a_A_log, a_dt_bias, hp):
    h0 = 2 * hp
    qc = np.arange(h0 * 128, (h0 + 2) * 128)
    kc = 1024 + qc
    vc = 2048 + np.arange(h0 * 256, (h0 + 2) * 256)
    zc = 4096 + np.arange(h0 * 256, (h0 + 2) * 256)
    bc = 6144 + np.arange(h0, h0 + 2)
    ac = 6152 + np.arange(h0, h0 + 2)
    cols = np.concatenate([qc, kc, vc, zc, bc, ac])
    w = np.ascontiguousarray(a_w_in[:, cols])
    conv = a_conv[:, np.concatenate([qc, kc, vc])]
    convT = np.ascontiguousarray(conv.reshape(4, 8, 128).transpose(2, 1, 0))
    gT = np.ascontiguousarray(a_norm.reshape(8, 128).T)
    gate_c = np.zeros((128, 2, 8), np.float32)
    gate_c[:, 0, :] = np.tile(a_dt_bias[h0:h0 + 2], 4)[None, :]
    gate_c[:, 1, :] = np.tile(a_A_log[h0:h0 + 2], 4)[None, :]
    return dict(w_in=w, convT=convT, gT=gT, gate_c=gate_c)


def prep_HB(b_w_in, b_norm, kv_norm, w_kv, q_gain, k_gain, sinks, rel_bias, q):
    w_qz = np.ascontiguousarray(np.concatenate([b_w_in[:, q * 512:(q + 1) * 512],
                                                b_w_in[:, 2048 + q * 512:2048 + (q + 1) * 512]], 1))
    wk = w_kv[:, q * 64:(q + 1) * 64]
    wv = w_kv[:, 256 + q * 64:256 + (q + 1) * 64]
    w_kv_c = np.ascontiguousarray(np.concatenate([wk, wk, wv], 1))
    hp_c = np.zeros((128, 8), np.float32)
    hp_c[:, 0] = np.tile(q_gain, 2)
    hp_c[:, 1] = np.tile(k_gain, 2)
    for j in range(4):
        hp_c[:64, 2 + j] = sinks[8 * q + 2 * j]
        hp_c[64:, 2 + j] = sinks[8 * q + 2 * j + 1]
    return dict(w_qz=w_qz, w_kv=w_kv_c, bgT=np.ascontiguousarray(b_norm.reshape(8, 128).T),
                kvgT=np.ascontiguousarray(kv_norm.reshape(8, 128).T), hp_c=hp_c,
                relb=np.ascontiguousarray(rel_bias[:, 8 * q:8 * q + 8]), onehot=_bucket_onehot())


def phase_OF(fw, nc, C, tag, T, xT, og_all, w_out_c, gainT, ssq_loc, ssq_all, hT_loc, groups,
             b_og, b_ssql, b_ssqa, b_hTl, on_chunk=None):
    NSB = T // 512
    do_proj = og_all is not None
    do_norm = hT_loc is not None
    big = [fw.ps(f"{tag}_big{i}", [128, 512], F32) for i in range(4)]
    if do_proj:
        Wo = fw.sb(f"{tag}_Wo", [128, 16, 256], BF16)
        wst = fw.sb(f"{tag}_wst", [128, 16, 256], F32)
        gsb = fw.sb(f"{tag}_gain", [128, 2], F32)
        ogt = [fw.sb(f"{tag}_ogt{i}", [128, 16, 512], BF16) for i in range(2)]
        fw.q_sp.dma(gsb.t[:], gainT, writes=[gsb.b])
        fw.q_pool.dma(wst.t[:], w_out_c.rearrange("(r p) f -> p r f", p=128), writes=[wst.b])
        for r in range(16):
            fw.act_(Wo.t[:, r, :], wst.t[:, r, :], AF.Copy, [wst.b, gsb.b], [Wo.b], scale=gsb.t[:, (r % 2):(r % 2) + 1])
    if do_norm:
        sq = [fw.sb(f"{tag}_sq{i}", [128, 512], F32) for i in range(2)]
        srow = fw.sb(f"{tag}_srow", [1, T], F32)
        parts = fw.sb(f"{tag}_parts", [4, T], F32)
        lnt = [fw.sb(f"{tag}_ln{i}", [128, 512], F32) for i in range(2)]
        hsb = [fw.sb(f"{tag}_hsb{i}", [128, 2, 512], BF16) for i in range(2)]

    def load_og(sb_):
        dst = ogt[sb_ % 2]
        fw.q_sp.dma(dst.t[:], og_all[sb_].rearrange("(r p) t -> p r t", p=128), reads=[b_og[sb_]], writes=[dst.b])

    if do_proj:
        load_og(0)
    for sb_ in range(NSB):
        cols = slice(sb_ * 512, (sb_ + 1) * 512)
        if do_proj:
            if sb_ + 1 < NSB:
                load_og(sb_ + 1)
            o_ = ogt[sb_ % 2]
            for fc in range(2):
                pb = big[fc]
                for r in range(16):
                    fw.mm(pb.t[:], Wo.t[:, r, fc * 128:(fc + 1) * 128], o_.t[:, r, :], r == 0, r == 15, [Wo.b, o_.b], [pb.b])
                fw.tt(xT.t[:, fc, cols], pb.t[:], xT.t[:, fc, cols], ALU.add, [pb.b, xT.b], [xT.b])
        if do_norm:
            pb = big[2 + sb_ % 2]
            for fc in range(2):
                fw.act_(sq[fc].t[:], xT.t[:, fc, cols], AF.Square, [xT.b], [sq[fc].b])
                fw.mm(pb.t[:], C.ONES, sq[fc].t[:], fc == 0, fc == 1, [C.c32.b, sq[fc].b], [pb.b])
            fw.act_(srow.t[0:1, cols], pb.t[0:1, :], AF.Copy, [pb.b], [srow.b])
    if not do_norm:
        return
    fw.q_pool.dma(ssq_loc, srow.t[:], reads=[srow.b], writes=[b_ssql])
    fw.collective("AllGather", ssq_loc, ssq_all, groups, reads=[b_ssql], writes=[b_ssqa])
    fw.q_pool.dma(parts.t[:], ssq_all, reads=[b_ssqa], writes=[parts.b])
    for sb_ in range(NSB):
        cols = slice(sb_ * 512, (sb_ + 1) * 512)
        pb, ln_, h_ = big[sb_ % 2], lnt[sb_ % 2], hsb[sb_ % 2]
        fw.mm(pb.t[:], C.ONES[0:4, :], parts.t[:, cols], True, True, [C.c32.b, parts.b], [pb.b])
        fw.act_(ln_.t[:], pb.t[:], AF.Ln, [pb.b, C.eps.b], [ln_.b], bias=C.eps.t[:], scale=1.0 / D)
        fw.act_(ln_.t[:], ln_.t[:], AF.Exp, [ln_.b], [ln_.b], scale=-0.5)
        for fc in range(2):
            fw.tt(h_.t[:, fc, :], xT.t[:, fc, cols], ln_.t[:], ALU.mult, [xT.b, ln_.b], [h_.b])
        fw.q_pool.dma(hT_loc[sb_].rearrange("(c p) t -> p c t", p=128), h_.t[:], reads=[h_.b], writes=[b_hTl[sb_]])
        if on_chunk is not None:
            on_chunk(sb_)


GROUPS = [[0, 1, 2, 3], [4, 5, 6, 7]]


def build_fused(T=SEQ):
    nc = bass.Bass("TRN2", target_bir_lowering=False)
    I = lambda n, s, d: nc.dram_tensor(n, s, d, kind="ExternalInput").ap()
    c32_ap, cbf_ap = I("c32", [128, 896], F32), I("cbf", [128, 384], BF16)
    xT_in = I("xT_in", [256, T], F32)
    A_in = [dict(w_in=I(f"a{l}_w_in", [D, 1540], F32), convT=I(f"a{l}_convT", [128, 8, 4], F32),
                 gT=I(f"a{l}_gT", [128, 8], F32), gate_c=I(f"a{l}_gate_c", [128, 2, 8], F32)) for l in range(2)]
    B_in = [dict(w_qz=I(f"b{j}_w_qz", [D, 1024], F32), bgT=I(f"b{j}_bgT", [128, 8], F32),
                 hp_c=I(f"b{j}_hp_c", [128, 8], F32)) for j in range(2)]
    w_kv, kvgT = I("w_kv_c", [D, 192], F32), I("kvgT", [128, 8], F32)
    relb, onehot = I("relb", [32, 8], F32), I("onehot", [32, 128], F32)
    Wout = [I(f"w_out{l}", [2048, 256], F32) for l in range(4)]
    gainT = [I(f"gainT{l}", [128, 2], F32) for l in range(4)]
    xT_out = nc.dram_tensor("xT_out", [256, T], F32, kind="ExternalOutput").ap()
    Dr = lambda n, s, d: nc.dram_tensor(n, s, d).ap()
    ssq_loc = [Dr(f"ssq_loc{l}", [1, T], F32) for l in range(4)]
    ssq_all = [Dr(f"ssq_all{l}", [4, T], F32) for l in range(4)]
    NCH = T // 512
    hT_loc = [[Dr(f"hT_loc{l}_{i}", [256, 512], BF16) for i in range(NCH)] for l in range(4)]
    hT_all = [[Dr(f"hT_all{l}_{i}", [1024, 512], BF16) for i in range(NCH)] for l in range(4)]
    ogT = [[Dr(f"ogT{l}_{i}", [512, 512], BF16) for i in range(NCH)] for l in range(4)]
    og_all = [[Dr(f"og_all{l}_{i}", [2048, 512], BF16) for i in range(NCH)] for l in range(4)]
    scratch = [Dr(f"scr{j}", [8 * 128 * 385 + 1024], F32) for j in range(2)]
    kv_dram = dict(ktab=Dr("kv_ktab", [128, T + 128], BF16), vpad=Dr("kv_vpad", [128, (T // 128 + 1) * 192], BF16),
                   ebt=Dr("kv_ebt", [128, 2048], BF16), buf=Buf("kvdram"), buf_ebt=Buf("ebtdram"))
    with ExitStack() as st:
        fw = FW(nc, st)
        C = Consts(fw, nc, c32_ap, cbf_ap)
        xT = fw.sb("xT_res", [128, 2, T], F32)
        fw.q_sp.dma(xT.t[:], xT_in.rearrange("(c p) t -> p c t", p=128), writes=[xT.b])
        b_hTall = [[Buf(f"hTall{l}_{i}") for i in range(NCH)] for l in range(4)]
        b_ogall = [[Buf(f"ogall{l}_{i}") for i in range(NCH)] for l in range(4)]

        def of_phase(l, proj_from):
            with ExitStack() as sc:
                fw.scope = sc
                b1, b2 = Buf("ssql"), Buf("ssqa")
                b3 = [Buf(f"hTl{i}") for i in range(NCH)]
                last = (l == 4)

                def oc(i):
                    fw.collective("AllGather", hT_loc[l][i], hT_all[l][i], GROUPS, reads=[b3[i]], writes=[b_hTall[l][i]])

                phase_OF(fw, nc, C, f"OF{l}", T, xT,
                         og_all[proj_from] if proj_from is not None else None,
                         Wout[proj_from] if proj_from is not None else None,
                         gainT[proj_from] if proj_from is not None else None,
                         None if last else ssq_loc[l], None if last else ssq_all[l], None if last else hT_loc[l], GROUPS,
                         b_ogall[proj_from] if proj_from is not None else None, b1, b2, b3, on_chunk=oc)
                fw.barrier()
            fw.scope = st

        def h_phase(l):
            with ExitStack() as sc:
                fw.scope = sc
                b_o = [Buf(f"ogT{i}") for i in range(NCH)]

                def oc(i):
                    fw.collective("AllGather", ogT[l][i], og_all[l][i], GROUPS, reads=[b_o[i]], writes=[b_ogall[l][i]])

                if l < 2:
                    a = A_in[l]
                    phase_H(fw, nc, C, f"HA{l}", hT_all[l], T, a["w_in"], a["convT"], a["gT"], a["gate_c"], ogT[l],
                            b_hTall[l], b_o, on_chunk=oc)
                else:
                    bb = B_in[l - 2]
                    phase_HB(fw, nc, C, f"HB{l}", T, hT_all[l], hT_all[2], bb["w_qz"], w_kv, bb["bgT"], kvgT, bb["hp_c"], relb,
                             onehot, scratch[l - 2], ogT[l], b_hTall[l], b_hTall[2], b_o, on_chunk=oc,
                             kv_dram=kv_dram, kv_load=(l == 3))
                fw.barrier()
            fw.scope = st

        with ExitStack() as sc0:
            fw.scope = sc0
            build_ebt(fw, nc, C, "EB", relb, onehot, scratch[0], kv_dram["ebt"], kv_dram["buf_ebt"])
            fw.barrier()
        fw.scope = st
        of_phase(0, None)
        for l in range(4):
            h_phase(l)
            of_phase(l + 1, l)
        b_out = Buf("xout")
        fw.q_pool.dma(xT_out.rearrange("(c p) t -> p c t", p=128), xT.t[:], reads=[xT.b], writes=[b_out])
        fw.finish([b_out])
        build_fused.counts = {e.name: e.ninst for e in (fw.pe, fw.dve, fw.act, fw.pool, fw.sp)}
    return nc


def make_in_maps(inputs, T=SEQ):
    f = lambda a: np.ascontiguousarray(np.asarray(a, dtype=np.float32))
    p = {k: f(v) for k, v in inputs.items()}
    c32, cbf = const_tables()
    oh = _bucket_onehot()
    ones_gain = np.ones((128, 2), np.float32)
    maps = []
    for c in range(8):
        b, q = c // 4, c % 4
        m = dict(c32=c32, cbf=cbf, onehot=oh)
        m["xT_in"] = np.ascontiguousarray(p["x"][b, :T, q * 256:(q + 1) * 256].T)
        for l in range(2):
            d = prep_HA(p["a_w_in"][l], p["a_conv"][l], p["a_norm"][l], p["a_A_log"][l], p["a_dt_bias"][l], q)
            for k, v in d.items():
                m[f"a{l}_{k}"] = v
            m[f"w_out{l}"] = np.ascontiguousarray(p["a_w_out"][l][:, q * 256:(q + 1) * 256])
            m[f"gainT{l}"] = np.ascontiguousarray(p["a_o_gain"][l].reshape(2, 128).T)
        for j in range(2):
            d = prep_HB(p["b_w_in"][j], p["b_norm"][j], p["kv_norm"], p["w_kv"], p["b_q_gain"][j], p["k_gain"],
                        p["b_sinks"][j], p["rel_bias"], q)
            m[f"b{j}_w_qz"], m[f"b{j}_bgT"], m[f"b{j}_hp_c"] = d["w_qz"], d["bgT"], d["hp_c"]
            m["w_kv_c"], m["kvgT"], m["relb"] = d["w_kv"], d["kvgT"], d["relb"]
            m[f"w_out{2 + j}"] = np.ascontiguousarray(p["b_w_out"][j][:, q * 256:(q + 1) * 256])
            m[f"gainT{2 + j}"] = ones_gain
        maps.append(m)
    return maps


_NC = {}


def kernel(**inputs):
    if "nc" not in _NC:
        _NC["nc"] = build_fused(SEQ)
    res = run_bass_kernel_spmd(_NC["nc"], make_in_maps(inputs, SEQ), core_ids=list(range(8)))
    out = np.zeros((NB, SEQ, D), np.float32)
    for c in range(8):
        b, q = c // 4, c % 4
        out[b, :, q * 256:(q + 1) * 256] = np.asarray(res.results[c]["xT_out"]).T
    return out
```

```python
import numpy as np
import ml_dtypes
from contextlib import ExitStack
import concourse.bass as bass
import concourse.mybir as mybir
from concourse.bass_utils import run_bass_kernel_spmd

F32 = mybir.dt.float32
BF16 = mybir.dt.bfloat16
AF = mybir.ActivationFunctionType
ALU = mybir.AluOpType
AX = mybir.AxisListType

D = 1024
SEQ = 8192
NB = 2
TS = 2048
EPS = 1e-6
NEG = -1.0e5


class Buf:
    __slots__ = ("name", "w", "r")

    def __init__(self, name=""):
        self.name = name
        self.w = None
        self.r = {}


class Tok:
    __slots__ = ("key", "sem", "val", "eng")

    def __init__(self, key, sem, val, eng=None):
        self.key, self.sem, self.val, self.eng = key, sem, val, eng


class Eng:
    def __init__(self, name, handle, sem, order_raw=True):
        self.name, self.h, self.sem = name, handle, sem
        self.count = 0
        self.waited = {}
        self.order_raw = order_raw
        self.ninst = 0

    def _wait(self, tok):
        if tok is None:
            return
        if tok.eng is self and not self.order_raw:
            return
        if self.waited.get(tok.key, 0) >= tok.val:
            return
        if tok.eng is not None and tok.val > tok.eng.count:
            raise RuntimeError(f"wait on pending token {tok.key} {tok.val}>{tok.eng.count} from {self.name}")
        self.h.wait_ge(tok.sem, tok.val)
        self.waited[tok.key] = tok.val

    def deps(self, reads, writes):
        for b in reads:
            if b.w is not None:
                self._wait(b.w)
        for b in writes:
            if b.w is not None and (b.w.eng is not self or self.order_raw):
                self._wait(b.w)
            for t in b.r.values():
                if t.eng is not self or self.order_raw:
                    self._wait(t)

    def op(self, ins_fn, reads=(), writes=(), sig=True):
        self.deps(reads, writes)
        ins = ins_fn()
        self.ninst += 1
        if sig:
            ins.then_inc(self.sem, 1)
            self.count += 1
            tok = Tok(self.name, self.sem, self.count, self)
        else:
            tok = Tok(self.name, self.sem, self.count + 1, self)
        for b in reads:
            b.r[self.name] = tok
        for b in writes:
            b.w = tok
            b.r = {}
        return tok


class DmaQ:
    def __init__(self, eng, sems, name):
        self.eng, self.sems, self.name = eng, sems, name
        self.n = 0
        self.counts = [0] * len(sems)
        self.toks = [None] * len(sems)

    def dma(self, out, in_, reads=(), writes=(), **kw):
        e = self.eng
        i = self.n % len(self.sems)
        self.n += 1
        e._wait(self.toks[i])
        e.deps(reads, writes)
        ins = e.h.dma_start(out=out, in_=in_, **kw)
        ins.then_inc(self.sems[i], 16)
        self.counts[i] += 16
        tok = Tok(f"{self.name}{i}", self.sems[i], self.counts[i], None)
        self.toks[i] = tok
        for b in reads:
            b.r[tok.key] = tok
        for b in writes:
            b.w = tok
            b.r = {}
        return tok


class TT:
    __slots__ = ("t", "b")

    def __init__(self, t, name):
        self.t = t
        self.b = Buf(name)


class FW:
    def __init__(self, nc, stack, ndma=8):
        self.nc = nc
        self.stack = stack
        S = lambda n: stack.enter_context(nc.semaphore(n))
        self.pe = Eng("pe", nc.tensor, S("s_pe"), order_raw=False)
        self.dve = Eng("dve", nc.vector, S("s_dve"))
        self.act = Eng("act", nc.scalar, S("s_act"))
        self.pool = Eng("pool", nc.gpsimd, S("s_pool"))
        self.sp = Eng("sp", nc.sync, S("s_sp"))
        self.q_sp = DmaQ(self.sp, [S(f"d_sp{i}") for i in range(ndma)], "qsp")
        self.q_pool = DmaQ(self.pool, [S(f"d_pl{i}") for i in range(ndma)], "qpl")
        self.cc_sem = S("s_cc")
        self.cc_count = 0
        self.scope = stack

    def sb(self, name, shape, dt):
        return TT(self.scope.enter_context(self.nc.sbuf_tensor(name, shape, dt)), name)

    def ps(self, name, shape, dt):
        return TT(self.scope.enter_context(self.nc.psum_tensor(name, shape, dt)), name)

    def barrier(self):
        engs = [self.pe, self.dve, self.act, self.pool, self.sp]
        toks = [Tok(e.name, e.sem, e.count, e) for e in engs if e.count > 0]
        for q in (self.q_sp, self.q_pool):
            toks += [t for t in q.toks if t is not None]
        for e in engs:
            for t in toks:
                if t.eng is not e:
                    e._wait(t)

    def collective(self, kind, ins_ap, outs_ap, groups, reads=(), writes=()):
        e = self.pool
        e.deps(reads, writes)
        ins = e.h.collective_compute(kind, ALU.bypass, replica_groups=groups,
                                     ins=[ins_ap], outs=[outs_ap])
        ins.then_inc(self.cc_sem, 1)
        self.cc_count += 1
        tok = Tok("cc", self.cc_sem, self.cc_count, None)
        for b in reads:
            b.r[tok.key] = tok
        for b in writes:
            b.w = tok
            b.r = {}
        return tok

    def finish(self, bufs):
        for b in bufs:
            self.pool._wait(b.w)

    def mm(self, out, lhsT, rhs, start, stop, R, W, sig=None):
        nc = self.nc
        if sig is None:
            sig = stop
        return self.pe.op(lambda: nc.tensor.matmul(out, lhsT, rhs, start=start, stop=stop), R, W, sig=sig)

    def tr(self, out, in_, ident, R, W, sig=True):
        nc = self.nc
        return self.pe.op(lambda: nc.tensor.transpose(out, in_, ident), R, W, sig=sig)

    def act_(self, out, in_, func, R, W, bias=None, scale=None, accum_out=None):
        nc = self.nc
        kw = {}
        if bias is not None:
            kw["bias"] = bias
        if scale is not None:
            kw["scale"] = scale
        if accum_out is not None:
            kw["accum_out"] = accum_out
        return self.act.op(lambda: nc.scalar.activation(out=out, in_=in_, func=func, **kw), R, W)

    def tt(self, out, in0, in1, op, R, W, eng=None):
        e = eng or self.dve
        return e.op(lambda: e.h.tensor_tensor(out=out, in0=in0, in1=in1, op=op), R, W)

    def ts(self, out, in0, s1, op0, R, W, s2=None, op1=None, eng=None):
        e = eng or self.dve
        if op1 is None:
            return e.op(lambda: e.h.tensor_scalar(out=out, in0=in0, scalar1=s1, scalar2=None, op0=op0), R, W)
        return e.op(lambda: e.h.tensor_scalar(out=out, in0=in0, scalar1=s1, scalar2=s2, op0=op0, op1=op1), R, W)

    def stt(self, out, in0, scalar, in1, op0, op1, R, W):
        nc = self.nc
        return self.dve.op(lambda: nc.vector.scalar_tensor_tensor(out=out, in0=in0, scalar=scalar, in1=in1,
                                                                  op0=op0, op1=op1), R, W)

    def cp(self, out, in_, R, W, eng=None):
        e = eng or self.dve
        return e.op(lambda: e.h.tensor_copy(out=out, in_=in_), R, W)

    def recip(self, out, in_, R, W):
        nc = self.nc
        return self.dve.op(lambda: nc.vector.reciprocal(out=out, in_=in_), R, W)

    def memset(self, ap, val, W, eng=None):
        e = eng or self.dve
        return e.op(lambda: e.h.memset(ap, val), (), W)


def const_tables():
    i = np.arange(128)
    same = (i[:, None] // 64) == (i[None, :] // 64)
    c32 = np.zeros((128, 7, 128), np.float32)
    c32[:, 0, :] = np.eye(128)
    c32[:, 1, :] = (same & (i[:, None] <= i[None, :]))
    c32[:, 2, :] = same
    c32[:, 3, :] = 1.0
    c32[:, 4, :] = np.where(same & (i[None, :] > i[:, None]), 0.0, NEG)
    c32[:, 5, :] = np.where(same & (i[None, :] >= i[:, None]), 0.0, NEG)
    c32[:, 6, :] = 0.0
    cbf = np.zeros((128, 3, 128), np.float32)
    cbf[:, 0, :] = np.eye(128)
    cbf[:, 1, :] = 1.0
    cbf[:, 2, :] = same
    return c32.reshape(128, 7 * 128), cbf.reshape(128, 384).astype(ml_dtypes.bfloat16)


class Consts:
    def __init__(self, fw, nc, c32_ap, cbf_ap):
        self.c32 = fw.sb("c32_sb", [128, 7, 128], F32)
        self.cbf = fw.sb("cbf_sb", [128, 3, 128], BF16)
        self.eps = fw.sb("c_eps", [128, 1], F32)
        self.lnq = fw.sb("c_lnq", [128, 1], F32)
        self.zero = fw.sb("c_zero", [128, 1], F32)
        fw.q_sp.dma(self.c32.t[:].rearrange("p a b -> p (a b)"), c32_ap, writes=[self.c32.b])
        fw.q_sp.dma(self.cbf.t[:].rearrange("p a b -> p (a b)"), cbf_ap, writes=[self.cbf.b])
        fw.memset(self.eps.t[:], EPS, [self.eps.b])
        fw.memset(self.lnq.t[:], -0.5 * float(np.log(128.0)), [self.lnq.b])
        fw.memset(self.zero.t[:], 0.0, [self.zero.b])
        t = self.c32.t
        self.I32, self.TRI, self.BONES, self.ONES = t[:, 0, :], t[:, 1, :], t[:, 2, :], t[:, 3, :]
        self.MSU, self.MU = t[:, 4, :], t[:, 5, :]
        self.Ibf, self.ONESbf, self.BONESbf = self.cbf.t[:, 0, :], self.cbf.t[:, 1, :], self.cbf.t[:, 2, :]


def _rr(lanes, extra=None, extra_turns=2):
    lanes = list(lanes)
    while lanes:
        for g in list(lanes):
            try:
                next(g)
            except StopIteration:
                lanes.remove(g)
        if extra is not None and extra[0] is not None:
            for _ in range(extra_turns):
                try:
                    next(extra[0])
                except StopIteration:
                    extra[0] = None
                    break


H_LANES = [4]
B2_STOP = [9]


def phase_H(fw, nc, C, tag, hT_ch, T, w_in, convT, gT, gate_c, ogT_ch, b_hT, b_ogT, on_chunk=None):
    NSC = T // 512
    WC = 1540
    NPAIR = NSC * 8
    W = fw.sb(tag + "_W", [128, 8, WC], BF16)
    cv = fw.sb(tag + "_conv", [128, 8, 4], F32)
    g_sb = fw.sb(tag + "_g", [128, 8], F32)
    gc = fw.sb(tag + "_gc", [128, 2, 8], F32)
    negA = fw.sb(tag + "_negA", [128, 8], F32)
    fw.q_sp.dma(cv.t[:], convT, writes=[cv.b])
    fw.q_sp.dma(g_sb.t[:], gT, writes=[g_sb.b])
    fw.q_sp.dma(gc.t[:], gate_c, writes=[gc.b])
    outer = fw.scope
    with ExitStack() as wsc:
        fw.scope = wsc
        wst = [fw.sb(f"{tag}_wst{i}", [128, WC], F32) for i in range(2)]
        for c in range(8):
            s = wst[c % 2]
            fw.q_sp.dma(s.t[:], w_in[c * 128:(c + 1) * 128, :], writes=[s.b])
            fw.act_(W.t[:, c, :], s.t[:], AF.Copy, [s.b, g_sb.b], [W.b], scale=g_sb.t[:, c:c + 1])
        fw.barrier()
    fw.scope = outer
    hT = [fw.sb(f"{tag}_hT{i}", [128, 8, 512], BF16) for i in range(2)]
    hist = fw.sb(tag + "_hist", [128, 8, 3], F32)
    pre = [fw.sb(f"{tag}_pre{i}", [128, 515], F32) for i in range(2)]
    cvt = [fw.sb(f"{tag}_cvt{i}", [128, 512], F32) for i in range(2)]
    sl = [fw.sb(f"{tag}_sl{g}", [128, 512], BF16) for g in range(4)]
    sqb = [fw.sb(f"{tag}_sqb{i}", [128, 512], BF16) for i in range(2)]
    lnt = [fw.sb(f"{tag}_lnt{i}", [128, 512], F32) for i in range(2)]
    qkn = [[fw.sb(f"{tag}_qkn{p}_{g}", [128, 512], BF16) for g in range(4)] for p in range(2)]
    vT = [[fw.sb(f"{tag}_vT{p}_{g}", [128, 512], BF16) for g in range(4)] for p in range(2)]
    zs = [[fw.sb(f"{tag}_zs{p}_{i}", [128, 512], BF16) for i in range(4)] for p in range(2)]
    GN = "eb beta nlnb ad gtm Gs eG bg kdS tmp8 kdS0 kdS1".split()
    GT = [{n: fw.sb(f"{tag}_{n}{p}", [128, 4, 2], F32) for n in GN} for p in range(2)]
    S32 = [fw.sb(f"{tag}_S32_{h}", [128, 256], F32) for h in range(2)]
    Sbf = [[fw.sb(f"{tag}_Sbf_{h}_{j}", [128, 256], BF16) for j in range(2)] for h in range(2)]
    ogTs = [fw.sb(f"{tag}_ogT{i}", [128, 4, 512], BF16) for i in range(2)]
    NP = 3
    PT = lambda n, shp, dt: [fw.sb(f"{tag}_{n}{i}", shp, dt) for i in range(NP)]
    Rt, T2, E2 = PT("Rt", [128, 256], F32), PT("T2", [128, 256], F32), PT("E2", [128, 256], F32)
    EGB, NA = PT("EGB", [128, 128], F32), PT("NA", [128, 256], BF16)
    XX = [[fw.sb(f"{tag}_XX{i}_{j}", [128, 384], BF16) for j in range(2)] for i in range(NP)]
    PP = PT("TT", [128, 128], BF16)
    RHS, kd, uw = PT("RHS", [128, 384], BF16), PT("kd", [128, 256], BF16), PT("uw", [128, 384], BF16)
    WkT, qd, QeT = PT("WkT", [128, 256], BF16), PT("qd", [128, 128], F32), PT("QeT", [128, 4, 64], BF16)
    ss, og = PT("ss", [128, 1], F32), PT("og", [128, 256], BF16)
    junk = fw.sb(tag + "_junk", [128, 256], F32)
    big = [fw.ps(f"{tag}_big0", [128, 512], F32)]
    otb = fw.ps(tag + "_otb", [128, 1024], BF16)
    gbkq = fw.ps(tag + "_gbkq", [128, 512], F32)
    neu = fw.ps(tag + "_neu", [128, 512], F32)
    solq = fw.ps(tag + "_solq", [128, 512], F32)
    tpo = fw.ps(tag + "_tpo", [128, 1024], BF16)
    poB = fw.ps(tag + "_poB", [128, 512], F32)
    pSB = fw.ps(tag + "_pSB", [128, 512], F32)
    xb = big[0]
    gp = xb.t[:, 0:16].rearrange("p (a b) -> p a b", b=4)
    gg = xb.t[:, 16:32].rearrange("p (a b) -> p a b", b=4)

    fw.act_(negA.t[:], gc.t[:, 1, :], AF.Exp, [gc.b], [negA.b])
    fw.ts(negA.t[:], negA.t[:], -1.0, ALU.mult, [negA.b], [negA.b])
    fw.memset(hist.t[:], 0.0, [hist.b])
    for i in range(NP):
        fw.memset(QeT[i].t[:], 0.0, [QeT[i].b])
    for h in range(2):
        fw.memset(S32[h].t[:], 0.0, [S32[h].b])
        fw.memset(Sbf[h][0].t[:], 0.0, [Sbf[h][0].b])
    negA3 = negA.t[:].rearrange("p (a b) -> p a b", b=2)
    dtb3 = gc.t[:, 0, :].rearrange("p (a b) -> p a b", b=2)

    def load_hT(sc):
        dst = hT[sc % 2]
        fw.q_sp.dma(dst.t[:], hT_ch[sc].rearrange("(c p) t -> p c t", p=128), reads=[b_hT[sc]], writes=[dst.b])

    def lane_P(sc):
        p = sc % 2
        G = GT[p]
        if sc + 1 < NSC:
            load_hT(sc + 1)
        h_ = hT[p]
        for g in range(8):
            pb = big[0]
            for c in range(8):
                fw.mm(pb.t[:], W.t[:, c, g * 128:(g + 1) * 128], h_.t[:, c, :], c == 0, c == 7, [W.b, h_.b], [pb.b])
                if c % 4 == 3:
                    yield
            P_ = pre[g % 2]
            fw.cp(P_.t[:, 0:3], hist.t[:, g, :], [hist.b], [P_.b], eng=fw.pool)
            fw.act_(P_.t[:, 3:515], pb.t[:], AF.Copy, [pb.b], [P_.b]); yield
            fw.cp(hist.t[:, g, :], P_.t[:, 512:515], [P_.b], [hist.b], eng=fw.pool)
            cb = cvt[g % 2]
            fw.ts(cb.t[:], P_.t[:, 0:512], cv.t[:, g, 0:1], ALU.mult, [P_.b, cv.b], [cb.b]); yield
            for j in range(1, 4):
                fw.stt(cb.t[:], P_.t[:, j:j + 512], cv.t[:, g, j:j + 1], cb.t[:], ALU.mult, ALU.add,
                       [P_.b, cv.b, cb.b], [cb.b]); yield
            dst = sl[g] if g < 4 else vT[p][g - 4]
            fw.act_(dst.t[:], cb.t[:], AF.Silu, [cb.b], [dst.b]); yield
        for t in range(4):
            pb = big[0]
            for c in range(8):
                fw.mm(pb.t[:], h_.t[:, c, t * 128:(t + 1) * 128], W.t[:, c, 1024:1536], c == 0, c == 7, [W.b, h_.b], [pb.b])
                if c % 4 == 3:
                    yield
            fw.act_(zs[p][t].t[:], pb.t[:], AF.Silu, [pb.b], [zs[p][t].b]); yield
        for g in range(4):
            sq, ln_, pb = sqb[g % 2], lnt[g % 2], big[0]
            fw.act_(sq.t[:], sl[g].t[:], AF.Square, [sl[g].b], [sq.b]); yield
            fw.mm(pb.t[:], C.ONESbf, sq.t[:], True, True, [C.cbf.b, sq.b], [pb.b]); yield
            fw.act_(ln_.t[:], pb.t[:], AF.Ln, [pb.b, C.eps.b], [ln_.b], bias=C.eps.t[:]); yield
            if g < 2:
                fw.act_(ln_.t[:], ln_.t[:], AF.Exp, [ln_.b, C.lnq.b], [ln_.b], bias=C.lnq.t[:], scale=-0.5)
            else:
                fw.act_(ln_.t[:], ln_.t[:], AF.Exp, [ln_.b], [ln_.b], scale=-0.5)
            yield
            fw.tt(qkn[p][g].t[:], sl[g].t[:], ln_.t[:], ALU.mult, [sl[g].b, ln_.b], [qkn[p][g].b]); yield
        for t in range(4):
            for c in range(8):
                fw.mm(gp[:, t, :], h_.t[:, c, t * 128:(t + 1) * 128], W.t[:, c, 1536:1540], c == 0, c == 7, [W.b, h_.b], [xb.b])
            yield
        eb, beta, nlnb, ad, gtm, Gs = G["eb"], G["beta"], G["nlnb"], G["ad"], G["gtm"], G["Gs"]
        eG, bg, kdS, tmp8, kdS0, kdS1 = G["eG"], G["bg"], G["kdS"], G["tmp8"], G["kdS0"], G["kdS1"]
        fw.act_(eb.t[:], gp[:, :, 0:2], AF.Exp, [xb.b], [eb.b], scale=-1.0); yield
        fw.ts(eb.t[:], eb.t[:], 1.0, ALU.add, [eb.b], [eb.b]); yield
        fw.recip(beta.t[:], eb.t[:], [eb.b], [beta.b]); yield
        fw.act_(nlnb.t[:], eb.t[:], AF.Ln, [eb.b], [nlnb.b]); yield
        fw.ts(nlnb.t[:], nlnb.t[:], -1.0, ALU.mult, [nlnb.b], [nlnb.b]); yield
        fw.tt(ad.t[:], gp[:, :, 2:4], dtb3, ALU.add, [xb.b, gc.b], [ad.b]); yield
        fw.act_(ad.t[:], ad.t[:], AF.Exp, [ad.b], [ad.b]); yield
        fw.ts(ad.t[:], ad.t[:], 1.0, ALU.add, [ad.b], [ad.b]); yield
        fw.act_(ad.t[:], ad.t[:], AF.Ln, [ad.b], [ad.b]); yield
        fw.tt(gtm.t[:], ad.t[:], negA3, ALU.mult, [ad.b, negA.b], [gtm.b]); yield
        for t in range(4):
            fw.mm(gg[:, t, 0:2], C.TRI, gtm.t[:, t, :], True, True, [C.c32.b, gtm.b], [xb.b])
            fw.mm(gg[:, t, 2:4], C.BONES, gtm.t[:, t, :], True, True, [C.c32.b, gtm.b], [xb.b])
        yield
        fw.cp(Gs.t[:], gg[:, :, 0:2], [xb.b], [Gs.b]); yield
        fw.act_(eG.t[:], Gs.t[:], AF.Exp, [Gs.b], [eG.b]); yield
        fw.tt(bg.t[:], eG.t[:], beta.t[:], ALU.mult, [eG.b, beta.b], [bg.b]); yield
        fw.tt(tmp8.t[:], gg[:, :, 2:4], Gs.t[:], ALU.subtract, [xb.b, Gs.b], [tmp8.b]); yield
        fw.act_(kdS.t[:], tmp8.t[:], AF.Exp, [tmp8.b], [kdS.b]); yield
        fw.ts(kdS0.t[:], kdS.t[:], C.BONES[:, 0:1], ALU.mult, [kdS.b, C.c32.b], [kdS0.b]); yield
        fw.ts(kdS1.t[:], kdS.t[:], C.BONES[:, 127:128], ALU.mult, [kdS.b, C.c32.b], [kdS1.b]); yield

    def ctx(g):
        sc, r = divmod(g, 8)
        t, h = divmod(r, 2)
        return sc, sc % 2, t, h, g % NP, slice(t * 128, (t + 1) * 128)

    def lane_F(g):
        sc, p, t, h, i, tk = ctx(g)
        G = GT[p]
        qT, kT = qkn[p][h].t[:, tk], qkn[p][2 + h].t[:, tk]
        qb, kb = qkn[p][h].b, qkn[p][2 + h].b
        gtm, nlnb, Gs = G["gtm"], G["nlnb"], G["Gs"]
        fw.act_(Rt[i].t[:, 128:256], C.TRI, AF.Copy, [C.c32.b, gtm.b], [Rt[i].b], scale=gtm.t[:, t, h:h + 1]); yield
        fw.stt(Rt[i].t[:, 0:128], C.I32, nlnb.t[:, t, h:h + 1], Rt[i].t[:, 128:256], ALU.mult, ALU.add,
               [C.c32.b, nlnb.b, Rt[i].b], [Rt[i].b]); yield
        fw.mm(gbkq.t[:, 0:256], C.ONES, Rt[i].t[:], True, True, [C.c32.b, Rt[i].b], [gbkq.b]); yield
        fw.mm(gbkq.t[:, 256:384], kT, kT, True, True, [kb], [gbkq.b]); yield
        fw.mm(gbkq.t[:, 384:512], kT, qT, True, True, [kb, qb], [gbkq.b]); yield
        fw.stt(T2[i].t[:, 0:128], gbkq.t[:, 0:128], Gs.t[:, t, h:h + 1], C.MSU, ALU.subtract, ALU.add,
               [gbkq.b, Gs.b, C.c32.b], [T2[i].b]); yield
        fw.stt(T2[i].t[:, 128:256], gbkq.t[:, 128:256], Gs.t[:, t, h:h + 1], C.MU, ALU.subtract, ALU.add,
               [gbkq.b, Gs.b, C.c32.b], [T2[i].b]); yield
        fw.act_(E2[i].t[:], T2[i].t[:], AF.Exp, [T2[i].b], [E2[i].b]); yield
        fw.act_(EGB[i].t[:], gbkq.t[:, 128:256], AF.Exp, [gbkq.b], [EGB[i].b]); yield
        fw.tt(NA[i].t[:], gbkq.t[:, 256:512], E2[i].t[:], ALU.mult, [gbkq.b, E2[i].b], [NA[i].b]); yield
        X0, X1 = XX[i]
        fw.mm(gbkq.t[:, 256:384], NA[i].t[:, 0:128], C.Ibf, True, True, [NA[i].b, C.cbf.b], [gbkq.b]); yield
        fw.cp(X0.t[:, 0:128], NA[i].t[:, 0:128], [NA[i].b], [X0.b], eng=fw.pool); yield
        fw.act_(X0.t[:, 256:384], gbkq.t[:, 256:384], AF.Copy, [gbkq.b], [X0.b]); yield
        fw.tt(X1.t[:, 128:256], C.Ibf, NA[i].t[:, 0:128], ALU.subtract, [C.cbf.b, NA[i].b], [X1.b]); yield

    def lane_N(g):
        i = g % NP
        X0, X1 = XX[i]
        fw.mm(neu.t[:, 0:128], X0.t[:, 256:384], X0.t[:, 0:128], True, True, [X0.b], [neu.b], sig=False)
        fw.mm(neu.t[:, 256:384], X0.t[:, 0:128], X0.t[:, 256:384], True, True, [X0.b], [neu.b]); yield
        fw.act_(X1.t[:, 0:128], neu.t[:, 0:128], AF.Copy, [neu.b], [X1.b]); yield
        fw.cp(X1.t[:, 256:384], neu.t[:, 256:384], [neu.b], [X1.b]); yield
        cur, nxt = X1, X0
        for k in range(1, 5):
            if k < 4:
                fw.mm(neu.t[:, 0:256], cur.t[:, 256:384], cur.t[:, 0:256], True, True, [cur.b], [neu.b], sig=False)
            else:
                fw.mm(neu.t[:, 128:256], cur.t[:, 256:384], cur.t[:, 128:256], True, True, [cur.b], [neu.b], sig=False)
            fw.mm(neu.t[:, 256:384], cur.t[:, 0:128], cur.t[:, 256:384], True, True, [cur.b], [neu.b]); yield
            if k < 4:
                fw.act_(nxt.t[:, 0:128], neu.t[:, 0:128], AF.Copy, [neu.b], [nxt.b]); yield
            fw.tt(nxt.t[:, 128:256], neu.t[:, 128:256], cur.t[:, 128:256], ALU.add, [neu.b, cur.b], [nxt.b]); yield
            fw.cp(nxt.t[:, 256:384], neu.t[:, 256:384], [neu.b], [nxt.b]); yield
            cur, nxt = nxt, cur
        fw.mm(neu.t[:, 128:256], cur.t[:, 256:384], cur.t[:, 128:256], True, True, [cur.b], [neu.b]); yield
        fw.tt(PP[i].t[:], neu.t[:, 128:256], cur.t[:, 128:256], ALU.add, [neu.b, cur.b], [PP[i].b]); yield

    def lane_B1(g):
        sc, p, t, h, i, tk = ctx(g)
        G = GT[p]
        qT, kT = qkn[p][h].t[:, tk], qkn[p][2 + h].t[:, tk]
        qb, kb = qkn[p][h].b, qkn[p][2 + h].b
        v0, v1 = vT[p][2 * h], vT[p][2 * h + 1]
        TTt = PP[i]
        fw.tr(tpo.t[:, 0:128], kT, C.Ibf, [kb, C.cbf.b], [tpo.b], sig=False)
        fw.tr(tpo.t[:, 128:256], v0.t[:, tk], C.Ibf, [v0.b, C.cbf.b], [tpo.b], sig=False)
        fw.tr(tpo.t[:, 256:384], v1.t[:, tk], C.Ibf, [v1.b, C.cbf.b], [tpo.b]); yield
        fw.act_(RHS[i].t[:, 0:256], tpo.t[:, 128:384], AF.Copy, [tpo.b, G["beta"].b], [RHS[i].b],
                scale=G["beta"].t[:, t, h:h + 1]); yield
        fw.ts(RHS[i].t[:, 256:384], tpo.t[:, 0:128], G["bg"].t[:, t, h:h + 1], ALU.mult, [tpo.b, G["bg"].b], [RHS[i].b]); yield
        fw.ts(kd[i].t[:, 0:128], tpo.t[:, 0:128], G["kdS0"].t[:, t, h:h + 1], ALU.mult, [tpo.b, G["kdS0"].b], [kd[i].b]); yield
        fw.ts(kd[i].t[:, 128:256], tpo.t[:, 0:128], G["kdS1"].t[:, t, h:h + 1], ALU.mult, [tpo.b, G["kdS1"].b], [kd[i].b]); yield
        fw.mm(solq.t[:, 0:384], TTt.t[:], RHS[i].t[:], True, True, [TTt.b, RHS[i].b], [solq.b]); yield
        fw.act_(uw[i].t[:, 0:256], solq.t[:, 0:256], AF.Copy, [solq.b], [uw[i].b]); yield
        fw.ts(uw[i].t[:, 256:384], solq.t[:, 256:384], -1.0, ALU.mult, [solq.b], [uw[i].b]); yield
        fw.mm(solq.t[:, 0:256], uw[i].t[:, 256:384], kd[i].t[:], True, True, [uw[i].b, kd[i].b], [solq.b]); yield
        fw.act_(WkT[i].t[:], solq.t[:, 0:256], AF.Copy, [solq.b], [WkT[i].b]); yield
        fw.mm(solq.t[:, 384:512], uw[i].t[:, 256:384], NA[i].t[:, 128:256], True, True, [uw[i].b, NA[i].b], [solq.b]); yield
        fw.tt(qd[i].t[:], qT, EGB[i].t[:], ALU.mult, [qb, EGB[i].b], [qd[i].b]); yield
        fw.tt(QeT[i].t[:, 0:4:3, :], solq.t[:, 384:512].rearrange("p (a b) -> p a b", b=64),
              qd[i].t[:].rearrange("p (a b) -> p a b", b=64), ALU.add, [solq.b, qd[i].b], [QeT[i].b]); yield

    def lane_B2(g):
        sc, p, t, h, i, tk = ctx(g)
        osb = ogTs[p]
        S0, S1 = Sbf[h]
        po, pS = poB.t[:, 0:256], pSB.t[:, 0:256]
        Q2 = QeT[i].t[:].rearrange("p a b -> p (a b)")
        u_ = uw[i].t[:, 0:256]

        def s_update(j, Sin, Sout):
            fw.mm(pS, kd[i].t[:, j * 128:(j + 1) * 128], u_, True, False, [kd[i].b, uw[i].b], [pSB.b], sig=False)
            fw.mm(pS, WkT[i].t[:, j * 128:(j + 1) * 128], Sin.t[:], False, True, [WkT[i].b, Sin.b], [pSB.b])
            yield
            glc = EGB[i].t[:, 64 * j + 63:64 * j + 64]
            fw.stt(Sout.t[:], S32[h].t[:], glc, pS, ALU.mult, ALU.add, [S32[h].b, EGB[i].b, pSB.b], [Sout.b]); yield
            fw.stt(S32[h].t[:], S32[h].t[:], glc, pS, ALU.mult, ALU.add, [S32[h].b, EGB[i].b, pSB.b], [S32[h].b]); yield

        yield from s_update(0, S0, S1)
        if B2_STOP[0] <= 1:
            return
        fw.mm(po, NA[i].t[:, 128:256], u_, True, False, [NA[i].b, uw[i].b], [poB.b], sig=False)
        fw.mm(po, Q2[:, 0:128], S0.t[:], False, False, [QeT[i].b, S0.b], [poB.b], sig=False)
        fw.mm(po, Q2[:, 128:256], S1.t[:], False, True, [QeT[i].b, S1.b], [poB.b]); yield
        if B2_STOP[0] <= 2:
            return
        yield from s_update(1, S1, S0)
        if B2_STOP[0] <= 3:
            return

    def lane_C(g):
        sc, p, t, h, i, tk = ctx(g)
        osb = ogTs[p]
        po = poB.t[:, 0:256]
        fw.act_(junk.t[:], po, AF.Square, [poB.b], [junk.b, ss[i].b], accum_out=ss[i].t[:]); yield
        fw.act_(ss[i].t[:], ss[i].t[:], AF.Ln, [ss[i].b, C.eps.b], [ss[i].b], bias=C.eps.t[:], scale=1.0 / 256); yield
        fw.act_(ss[i].t[:], ss[i].t[:], AF.Exp, [ss[i].b], [ss[i].b], scale=-0.5); yield
        hc = slice(h * 256, (h + 1) * 256)
        fw.stt(og[i].t[:], po, ss[i].t[:], zs[p][t].t[:, hc], ALU.mult, ALU.mult, [poB.b, ss[i].b, zs[p][t].b], [og[i].b]); yield
        if B2_STOP[0] <= 4:
            return
        for v in range(2):
            oc = h * 2 + v
            fw.tr(otb.t[:, oc * 128:(oc + 1) * 128], og[i].t[:, v * 128:(v + 1) * 128], C.Ibf,
                  [og[i].b, C.cbf.b], [otb.b], sig=(v == 1))
        yield
        if B2_STOP[0] <= 5:
            return
        for v in range(2):
            oc = h * 2 + v
            fw.act_(osb.t[:, oc, tk], otb.t[:, oc * 128:(oc + 1) * 128], AF.Copy, [otb.b], [osb.b]); yield
        if g % 8 == 7:
            fw.q_pool.dma(ogT_ch[sc].rearrange("(a p) t -> p a t", p=128), osb.t[:], reads=[osb.b], writes=[b_ogT[sc]])
            if on_chunk is not None:
                on_chunk(sc)


    def lane_B(g):
        yield from lane_B1(g)
        if H_LANES[0] >= 4:
            yield from lane_B2(g)

    load_hT(0)
    for _ in lane_P(0):
        pass
    cur_P = [None]
    for step in range(NPAIR + 3):
        if step < NPAIR and step % 8 == 0 and cur_P[0] is not None:
            for _ in cur_P[0]:
                pass
            cur_P[0] = None
        if step % 8 == 3 and step // 8 + 1 < NSC:
            cur_P[0] = lane_P(step // 8 + 1)
        lanes = []
        if step < NPAIR:
            lanes.append(lane_F(step))
        if 0 <= step - 1 < NPAIR:
            lanes.append(lane_N(step - 1))
        if 0 <= step - 2 < NPAIR:
            lanes.append(lane_B(step - 2))
        if 0 <= step - 3 < NPAIR:
            lanes.append(lane_C(step - 3))
        _rr(lanes, cur_P, 2)
    if cur_P[0] is not None:
        for _ in cur_P[0]:
            pass
    if H_LANES[0] < 4 or B2_STOP[0] < 9:
        for sc in range(NSC):
            fw.q_pool.dma(ogT_ch[sc].rearrange("(a p) t -> p a t", p=128), ogTs[sc % 2].t[:], reads=[ogTs[sc % 2].b], writes=[b_ogT[sc]])


def phase_ON(fw, nc, C, tag, NTOK, x_in, x_out, og_src, w_out, gainT, hT_out, b_xin, b_xout, b_og, b_hTout):
    NT = NTOK // 128
    do_proj = og_src is not None
    do_norm = hT_out is not None
    xt = [fw.sb(f"{tag}_xt{i}", [128, 1024], F32) for i in range(2)]
    big = [fw.ps(f"{tag}_big{i}", [128, 512], F32) for i in range(4)]
    if do_proj:
        Wo = fw.sb(f"{tag}_Wo", [128, 16, 1024], BF16)
        wst = [fw.sb(f"{tag}_wst{i}", [128, 1024], F32) for i in range(2)]
        gsb = fw.sb(f"{tag}_gain", [128, 2], F32)
        ogt = [fw.sb(f"{tag}_ogt{i}", [128, 16, 512], BF16) for i in range(2)]
        fw.q_sp.dma(gsb.t[:], gainT, writes=[gsb.b])
        for r in range(16):
            s = wst[r % 2]
            fw.q_sp.dma(s.t[:], w_out[r * 128:(r + 1) * 128, :], writes=[s.b])
            fw.act_(Wo.t[:, r, :], s.t[:], AF.Copy, [s.b, gsb.b], [Wo.b], scale=gsb.t[:, (r % 2):(r % 2) + 1])
    if do_norm:
        sq = fw.sb(f"{tag}_sq", [128, 1024], F32)
        ss = [fw.sb(f"{tag}_ss{i}", [128, 1], F32) for i in range(2)]
        xb = [fw.sb(f"{tag}_xb{i}", [128, 1024], F32) for i in range(2)]
        hT = [fw.sb(f"{tag}_hT{i}", [128, 8, 512], BF16) for i in range(2)]

    def load_og(sb_):
        dst = ogt[sb_ % 2]
        fw.q_sp.dma(dst.t[:].rearrange("p (a b) t -> p a b t", a=4),
                    og_src.rearrange("a (b p) t -> p a b t", p=128)[:, :, :, sb_ * 512:(sb_ + 1) * 512],
                    reads=[b_og], writes=[dst.b])

    if do_proj:
        load_og(0)
    for t in range(NT):
        sb_, tt_ = t // 4, t % 4
        tk = slice(tt_ * 128, (tt_ + 1) * 128)
        if do_proj and tt_ == 0 and (sb_ + 1) * 4 < NT:
            load_og(sb_ + 1)
        X = xt[t % 2]
        fw.q_sp.dma(X.t[:], x_in[t * 128:(t + 1) * 128, :], reads=[b_xin], writes=[X.b])
        if do_proj:
            o_ = ogt[sb_ % 2]
            for half in range(2):
                pb = big[half]
                for r in range(16):
                    fw.mm(pb.t[:], o_.t[:, r, tk], Wo.t[:, r, half * 512:(half + 1) * 512], r == 0, r == 15,
                          [o_.b, Wo.b], [pb.b])
                fw.tt(X.t[:, half * 512:(half + 1) * 512], pb.t[:], X.t[:, half * 512:(half + 1) * 512], ALU.add,
                      [pb.b, X.b], [X.b])
        if x_out is not None:
            fw.q_pool.dma(x_out[t * 128:(t + 1) * 128, :], X.t[:], reads=[X.b], writes=[b_xout])
        if do_norm:
            s_ = ss[t % 2]
            B_ = xb[t % 2]
            H_ = hT[sb_ % 2]
            fw.act_(sq.t[:], X.t[:], AF.Square, [X.b], [sq.b, s_.b], accum_out=s_.t[:])
            fw.ts(s_.t[:], s_.t[:], 1.0 / D, ALU.mult, [s_.b], [s_.b], s2=EPS, op1=ALU.add)
            fw.act_(s_.t[:], s_.t[:], AF.Sqrt, [s_.b], [s_.b])
            fw.recip(s_.t[:], s_.t[:], [s_.b], [s_.b])
            fw.ts(B_.t[:], X.t[:], s_.t[:], ALU.mult, [X.b, s_.b], [B_.b])
            for g4 in range(2):
                pb = big[2 + g4]
                for cc in range(4):
                    c = g4 * 4 + cc
                    fw.tr(pb.t[:, cc * 128:(cc + 1) * 128], B_.t[:, c * 128:(c + 1) * 128], C.I32,
                          [B_.b, C.c32.b], [pb.b], sig=(cc == 3))
                for cc in range(4):
                    c = g4 * 4 + cc
                    if cc % 2 == 0:
                        fw.act_(H_.t[:, c, tk], pb.t[:, cc * 128:(cc + 1) * 128], AF.Copy, [pb.b], [H_.b])
                    else:
                        fw.cp(H_.t[:, c, tk], pb.t[:, cc * 128:(cc + 1) * 128], [pb.b], [H_.b])
            if tt_ == 3 or t == NT - 1:
                n_ = (tt_ + 1) * 128
                fw.q_pool.dma(hT_out.rearrange("(c p) t -> p c t", p=128)[:, :, sb_ * 512:sb_ * 512 + n_],
                              H_.t[:, :, 0:n_], reads=[H_.b], writes=[b_hTout])


def build_ebt(fw, nc, C, tag, relb, onehot, scratch, ebt_dram, b_ebt):
    rb = fw.sb(f"{tag}_rb", [32, 8], F32)
    oh = fw.sb(f"{tag}_oh", [32, 128], F32)
    EBT = fw.sb(f"{tag}_EBT", [128, 2, 8, 128], BF16)
    rhsE = fw.sb(f"{tag}_rhsE", [32, 1024], F32)
    Fsb = fw.sb(f"{tag}_Fsb", [128, 8, 384], F32)
    EBf = fw.sb(f"{tag}_EBf", [128, 2, 1024], F32)
    big = [fw.ps(f"{tag}_big{i}", [128, 512], F32) for i in range(2)]
    fw.q_sp.dma(rb.t[:], relb, writes=[rb.b])
    fw.q_sp.dma(oh.t[:], onehot, writes=[oh.b])
    for h in range(8):
        fw.ts(rhsE.t[:, h * 128:(h + 1) * 128], oh.t[:], rb.t[:, h:h + 1], ALU.mult, [oh.b, rb.b], [rhsE.b])
    fw.memset(Fsb.t[:], 0.0, [Fsb.b], eng=fw.pool)
    for k2 in range(2):
        fw.mm(big[k2].t[:], C.ONES[0:32, :], rhsE.t[:, k2 * 512:(k2 + 1) * 512], True, True, [C.c32.b, rhsE.b], [big[k2].b])
        for hh in range(4):
            h = k2 * 4 + hh
            fw.act_(Fsb.t[:, h, 128:256], big[k2].t[:, hh * 128:(hh + 1) * 128], AF.Exp, [big[k2].b], [Fsb.b])
    b_scr = Buf("scratch")
    wr = bass.AP(scratch.tensor, 0, [[385, 128], [128 * 385, 8], [1, 384]])
    fw.q_pool.dma(wr, Fsb.t[:], reads=[Fsb.b], writes=[b_scr])
    for kb in range(2):
        rd = bass.AP(scratch.tensor, 256 - 128 * kb, [[384, 128], [128 * 385, 8], [1, 128]])
        fw.q_pool.dma(EBf.t[:, kb, :].rearrange("p (h q) -> p h q", q=128), rd, reads=[b_scr], writes=[EBf.b])
    fw.cp(EBT.t[:].rearrange("p a h q -> p a (h q)"), EBf.t[:], [EBf.b], [EBT.b])
    fw.q_pool.dma(ebt_dram, EBT.t[:].rearrange("p a h q -> p (a h q)"), reads=[EBT.b], writes=[b_ebt])


def phase_HB(fw, nc, C, tag, T, hT_ch, hTkv_ch, w_qz, w_kv, bgT, kvgT, hp_c, relb, onehot, scratch,
             ogT_ch, b_hT, b_hTkv, b_ogT, on_chunk=None, kv_dram=None, kv_load=False):
    NSB = T // 512
    NBLK = T // 128
    Wqz = fw.sb(f"{tag}_Wqz", [128, 8, 1024], BF16)
    Wkv = fw.sb(f"{tag}_Wkv", [128, 8, 192], BF16)
    g1 = fw.sb(f"{tag}_g1", [128, 8], F32)
    g2 = fw.sb(f"{tag}_g2", [128, 8], F32)
    hp = fw.sb(f"{tag}_hp", [128, 8], F32)
    qg8 = fw.sb(f"{tag}_qg8", [128, 1], F32)
    qg8A = fw.sb(f"{tag}_qg8A", [128, 1], F32)
    qg8B = fw.sb(f"{tag}_qg8B", [128, 1], F32)
    esink = fw.sb(f"{tag}_esink", [128, 4], F32)
    EBT = fw.sb(f"{tag}_EBT", [128, 2, 8, 128], BF16)
    big = [fw.ps(f"{tag}_big{i}", [128, 512], F32) for i in range(2)]
    SA = [fw.ps(f"{tag}_SA{i}", [128, 512], F32) for i in range(2)]
    SBk = [fw.ps(f"{tag}_SB{i}", [128, 512], F32) for i in range(2)]
    oT = fw.ps(f"{tag}_oT", [128, 512], F32)
    den = fw.ps(f"{tag}_den", [128, 512], F32)

    fw.q_sp.dma(g1.t[:], bgT, writes=[g1.b])
    fw.q_sp.dma(g2.t[:], kvgT, writes=[g2.b])
    fw.q_sp.dma(hp.t[:], hp_c, writes=[hp.b])
    outer_w = fw.scope
    with ExitStack() as wsc:
        fw.scope = wsc
        wst = [fw.sb(f"{tag}_wst{i}", [128, 1024], F32) for i in range(2)]
        wst2 = [fw.sb(f"{tag}_wstk{i}", [128, 192], F32) for i in range(2)]
        for c in range(8):
            s = wst[c % 2]
            fw.q_sp.dma(s.t[:], w_qz[c * 128:(c + 1) * 128, :], writes=[s.b])
            fw.act_(Wqz.t[:, c, :], s.t[:], AF.Copy, [s.b, g1.b], [Wqz.b], scale=g1.t[:, c:c + 1])
            s2 = wst2[c % 2]
            fw.q_sp.dma(s2.t[:], w_kv[c * 128:(c + 1) * 128, :], writes=[s2.b])
            fw.act_(Wkv.t[:, c, :], s2.t[:], AF.Copy, [s2.b, g2.b], [Wkv.b], scale=g2.t[:, c:c + 1])
        fw.barrier()
    fw.scope = outer_w
    fw.ts(qg8.t[:], hp.t[:, 0:1], 0.125, ALU.mult, [hp.b], [qg8.b])
    fw.ts(qg8A.t[:], qg8.t[:], C.BONES[:, 0:1], ALU.mult, [qg8.b, C.c32.b], [qg8A.b])
    fw.ts(qg8B.t[:], qg8.t[:], C.BONES[:, 127:128], ALU.mult, [qg8.b, C.c32.b], [qg8B.b])
    fw.act_(esink.t[:], hp.t[:, 2:6], AF.Exp, [hp.b], [esink.b])
    fw.q_pool.dma(EBT.t[:].rearrange("p a h q -> p (a h q)"), kv_dram["ebt"], reads=[kv_dram["buf_ebt"]], writes=[EBT.b])
    KTAB = fw.sb(f"{tag}_KTAB", [128, T + 128], BF16)
    Vpad = fw.sb(f"{tag}_Vpad", [128, NBLK + 1, 192], BF16)
    OP = fw.sb(f"{tag}_OP", [128, 192], BF16)
    hq = [fw.sb(f"{tag}_hq{i}", [128, 8, 512], BF16) for i in range(2)]
    sqb = [fw.sb(f"{tag}_sqb{i}", [128, 512], BF16) for i in range(2)]
    lnt = [fw.sb(f"{tag}_lnt{i}", [128, 512], F32) for i in range(2)]
    QT = [fw.sb(f"{tag}_QT{j}", [128, 512], BF16) for j in range(4)]
    zsT = [fw.sb(f"{tag}_zsT{j}", [128, 512], BF16) for j in range(4)]
    PA = [fw.sb(f"{tag}_PA{i}", [128, 4, 128], BF16) for i in range(2)]
    PB = [fw.sb(f"{tag}_PB{i}", [128, 4, 128], BF16) for i in range(2)]
    lnd = fw.sb(f"{tag}_lnd", [128, 512], F32)
    rr = fw.sb(f"{tag}_rr", [128, 512], F32)
    rz = fw.sb(f"{tag}_rz", [128, 512], F32)
    osb = [fw.sb(f"{tag}_osb{i}", [128, 4, 512], BF16) for i in range(2)]
    if not kv_load:
        fw.memset(Vpad.t[:], 0.0, [Vpad.b], eng=fw.pool)
        fw.memset(KTAB.t[:, 0:128], 0.0, [KTAB.b], eng=fw.pool)
    fw.memset(OP.t[:], 0.0, [OP.b], eng=fw.pool)
    fw.memset(OP.t[:, 64:128], 1.0, [OP.b], eng=fw.pool)

    def load_h(src, bsrc, dst, sc):
        fw.q_sp.dma(dst.t[:], src[sc].rearrange("(c p) t -> p c t", p=128), reads=[bsrc[sc]], writes=[dst.b])

    def kv_for_sb(sc, h_):
        pk, pq = big[0], big[1]
        for c in range(8):
            fw.mm(pk.t[:], Wkv.t[:, c, 0:128], h_.t[:, c, :], c == 0, c == 7, [Wkv.b, h_.b], [pk.b])
            if c % 4 == 3:
                yield
        sq, ln_ = sqb[0], lnt[0]
        fw.act_(sq.t[:], pk.t[:], AF.Square, [pk.b], [sq.b]); yield
        fw.mm(pq.t[:], C.BONESbf, sq.t[:], True, True, [C.cbf.b, sq.b], [pq.b]); yield
        fw.act_(ln_.t[:], pq.t[:], AF.Ln, [pq.b, C.eps.b], [ln_.b], bias=C.eps.t[:], scale=1.0 / 64); yield
        fw.act_(ln_.t[:], ln_.t[:], AF.Exp, [ln_.b], [ln_.b], scale=-0.5); yield
        fw.stt(KTAB.t[:, 128 + sc * 512:128 + (sc + 1) * 512], pk.t[:], hp.t[:, 1:2], ln_.t[:], ALU.mult, ALU.mult,
               [pk.b, hp.b, ln_.b], [KTAB.b]); yield
        for t in range(4):
            for c in range(8):
                fw.mm(pq.t[:, t * 64:(t + 1) * 64], h_.t[:, c, t * 128:(t + 1) * 128], Wkv.t[:, c, 128:192],
                      c == 0, c == 7, [Wkv.b, h_.b], [pq.b])
            yield
        for t in range(4):
            fw.act_(Vpad.t[:, sc * 4 + t + 1, 64:128], pq.t[:, t * 64:(t + 1) * 64], AF.Copy, [pq.b], [Vpad.b]); yield

    if kv_load:
        fw.q_sp.dma(KTAB.t[:], kv_dram["ktab"], reads=[kv_dram["buf"]], writes=[KTAB.b])
        fw.q_pool.dma(Vpad.t[:].rearrange("p a b -> p (a b)"), kv_dram["vpad"], reads=[kv_dram["buf"]], writes=[Vpad.b])

    QT2 = [QT, [fw.sb(f"{tag}_QTb{j}", [128, 512], BF16) for j in range(4)]]
    zsT2 = [zsT, [fw.sb(f"{tag}_zsTb{j}", [128, 512], BF16) for j in range(4)]]
    PA2 = [PA, [fw.sb(f"{tag}_PAb{i}", [128, 4, 128], BF16) for i in range(2)]]
    PB2 = [PB, [fw.sb(f"{tag}_PBb{i}", [128, 4, 128], BF16) for i in range(2)]]
    psA = fw.sb(f"{tag}_psA", [128, 4, 128], BF16)
    psB = fw.sb(f"{tag}_psB", [128, 4, 128], BF16)

    def lane_Q(sc):
        p = sc % 2
        if sc + 1 < NSB:
            load_h(hT_ch, b_hT, hq[(sc + 1) % 2], sc + 1)
        h_ = hq[p]
        if not kv_load:
            yield from kv_for_sb(sc, h_)
        pb, p2 = big[0], big[1]
        for j in range(4):
            for c in range(8):
                fw.mm(pb.t[:], Wqz.t[:, c, j * 128:(j + 1) * 128], h_.t[:, c, :], c == 0, c == 7, [Wqz.b, h_.b], [pb.b])
                if c % 4 == 3:
                    yield
            sq, ln_ = sqb[j % 2], lnt[j % 2]
            fw.act_(sq.t[:], pb.t[:], AF.Square, [pb.b], [sq.b]); yield
            fw.mm(p2.t[:], C.BONESbf, sq.t[:], True, True, [C.cbf.b, sq.b], [p2.b]); yield
            fw.act_(ln_.t[:], p2.t[:], AF.Ln, [p2.b, C.eps.b], [ln_.b], bias=C.eps.t[:], scale=1.0 / 64); yield
            fw.act_(ln_.t[:], ln_.t[:], AF.Exp, [ln_.b], [ln_.b], scale=-0.5); yield
            fw.stt(QT2[p][j].t[:], pb.t[:], qg8.t[:], ln_.t[:], ALU.mult, ALU.mult, [pb.b, qg8.b, ln_.b], [QT2[p][j].b]); yield
        for j in range(4):
            for c in range(8):
                fw.mm(pb.t[:], Wqz.t[:, c, 512 + j * 128:512 + (j + 1) * 128], h_.t[:, c, :], c == 0, c == 7,
                      [Wqz.b, h_.b], [pb.b])
                if c % 4 == 3:
                    yield
            fw.act_(zsT2[p][j].t[:], pb.t[:], AF.Silu, [pb.b], [zsT2[p][j].b]); yield

    def lane_S(n):
        sc, tt_ = divmod(n, 4)
        Q_ = QT2[sc % 2]
        tk = slice(tt_ * 128, (tt_ + 1) * 128)
        kbs = [1] if n == 0 else [0, 1]
        for kb in kbs:
            kc = slice((n + kb) * 128, (n + kb + 1) * 128)
            sa, sbk, pa, pb_ = SA[kb], SBk[kb], PA2[n % 2][kb], PB2[n % 2][kb]
            for j in range(4):
                fw.mm(sa.t[:, j * 128:(j + 1) * 128], KTAB.t[0:64, kc], Q_[j].t[0:64, tk], True, True,
                      [KTAB.b, Q_[j].b], [sa.b], sig=(j == 3))
            yield
            for j in range(4):
                fw.mm(sbk.t[:, j * 128:(j + 1) * 128], KTAB.t[64:128, kc], Q_[j].t[64:128, tk], True, True,
                      [KTAB.b, Q_[j].b], [sbk.b], sig=(j == 3))
            yield
            paf = pa.t[:].rearrange("p a b -> p (a b)")
            pbf = pb_.t[:].rearrange("p a b -> p (a b)")
            fw.act_(paf, sa.t[:], AF.Exp, [sa.b], [pa.b]); yield
            fw.act_(pbf, sbk.t[:], AF.Exp, [sbk.b], [pb_.b]); yield
            fw.tt(pa.t[:], pa.t[:], EBT.t[:, kb, 0:8:2, :], ALU.mult, [pa.b, EBT.b], [pa.b]); yield
            fw.tt(pb_.t[:], pb_.t[:], EBT.t[:, kb, 1:8:2, :], ALU.mult, [pb_.b, EBT.b], [pb_.b]); yield

    def lane_O(n):
        sc, tt_ = divmod(n, 4)
        Z_ = zsT2[sc % 2]
        ob = osb[sc % 2]
        tk = slice(tt_ * 128, (tt_ + 1) * 128)
        kbs = [1] if n == 0 else [0, 1]
        first = True
        for kb in kbs:
            paf = PA2[n % 2][kb].t[:].rearrange("p a b -> p (a b)")
            pbf = PB2[n % 2][kb].t[:].rearrange("p a b -> p (a b)")
            fw.mm(oT.t[:], Vpad.t[:, n + kb, 64:192], paf, first, False, [Vpad.b, PA2[n % 2][kb].b], [oT.b], sig=False)
            fw.mm(oT.t[:], Vpad.t[:, n + kb, 0:128], pbf, False, kb == 1, [Vpad.b, PB2[n % 2][kb].b], [oT.b], sig=(kb == 1))
            first = False
        yield
        if len(kbs) == 2:
            fw.tt(psA.t[:], PA2[n % 2][0].t[:], PA2[n % 2][1].t[:], ALU.add, [PA2[n % 2][0].b, PA2[n % 2][1].b], [psA.b]); yield
            fw.tt(psB.t[:], PB2[n % 2][0].t[:], PB2[n % 2][1].t[:], ALU.add, [PB2[n % 2][0].b, PB2[n % 2][1].b], [psB.b]); yield
            sA, sB = psA, psB
        else:
            sA, sB = PA2[n % 2][1], PB2[n % 2][1]
        fw.mm(den.t[:], OP.t[:, 64:192], sA.t[:].rearrange("p a b -> p (a b)"), True, False, [OP.b, sA.b], [den.b], sig=False)
        fw.mm(den.t[:], OP.t[:, 0:128], sB.t[:].rearrange("p a b -> p (a b)"), False, True, [OP.b, sB.b], [den.b])
        yield
        for j in range(4):
            fw.act_(lnd.t[:, j * 128:(j + 1) * 128], den.t[:, j * 128:(j + 1) * 128], AF.Ln, [den.b, esink.b], [lnd.b],
                    bias=esink.t[:, j:j + 1]); yield
        fw.act_(rr.t[:], lnd.t[:], AF.Exp, [lnd.b], [rr.b], scale=-1.0); yield
        for j in range(4):
            fw.tt(rz.t[:, j * 128:(j + 1) * 128], rr.t[:, j * 128:(j + 1) * 128], Z_[j].t[:, tk], ALU.mult,
                  [rr.b, Z_[j].b], [rz.b], eng=fw.pool); yield
        fw.tt(ob.t[:, :, tk], oT.t[:].rearrange("p (a b) -> p a b", b=128), rz.t[:].rearrange("p (a b) -> p a b", b=128),
              ALU.mult, [oT.b, rz.b], [ob.b]); yield
        if tt_ == 3:
            fw.q_pool.dma(ogT_ch[sc].rearrange("(a p) t -> p a t", p=128), ob.t[:], reads=[ob.b], writes=[b_ogT[sc]])
            if on_chunk is not None:
                on_chunk(sc)

    load_h(hT_ch, b_hT, hq[0], 0)
    for _ in lane_Q(0):
        pass
    cur_Q = [None]
    for step in range(NBLK + 1):
        if step < NBLK and step % 4 == 0 and cur_Q[0] is not None:
            for _ in cur_Q[0]:
                pass
            cur_Q[0] = None
        if step % 4 == 1 and step // 4 + 1 < NSB:
            cur_Q[0] = lane_Q(step // 4 + 1)
        lanes = []
        if step < NBLK:
            lanes.append(lane_S(step))
        if 0 <= step - 1 < NBLK:
            lanes.append(lane_O(step - 1))
        _rr(lanes, cur_Q, 3)
    if cur_Q[0] is not None:
        for _ in cur_Q[0]:
            pass
    if kv_dram is not None and not kv_load:
        fw.q_pool.dma(kv_dram["ktab"], KTAB.t[:], reads=[KTAB.b], writes=[kv_dram["buf"]])
        fw.q_pool.dma(kv_dram["vpad"], Vpad.t[:].rearrange("p a b -> p (a b)"), reads=[Vpad.b], writes=[kv_dram["buf"]])


def _bucket_onehot():
    n = np.arange(128)
    max_exact = 16
    large = max_exact + (np.log(np.maximum(n, 1) / max_exact) / np.log(128 / max_exact) * (32 - max_exact)).astype(np.int64)
    large = np.minimum(large, 31)
    b = np.where(n < max_exact, n, large)
    oh = np.zeros((32, 128), np.float32)
    oh[b, n] = 1.0
    return oh


def prep_HA(a_w_in, a_conv, a_norm, a_A_log, a_dt_bias, hp):
    h0 = 2 * hp
    qc = np.arange(h0 * 128, (h0 + 2) * 128)
    kc = 1024 + qc
    vc = 2048 + np.arange(h0 * 256, (h0 + 2) * 256)
    zc = 4096 + np.arange(h0 * 256, (h0 + 2) * 256)
    bc = 6144 + np.arange(h0, h0 + 2)
    ac = 6152 + np.arange(h0, h0 + 2)
    cols = np.concatenate([qc, kc, vc, zc, bc, ac])
    w = np.ascontiguousarray(a_w_in[:, cols])
    conv = a_conv[:, np.concatenate([qc, kc, vc])]
    convT = np.ascontiguousarray(conv.reshape(4, 8, 128).transpose(2, 1, 0))
    gT = np.ascontiguousarray(a_norm.reshape(8, 128).T)
    gate_c = np.zeros((128, 2, 8), np.float32)
    gate_c[:, 0, :] = np.tile(a_dt_bias[h0:h0 + 2], 4)[None, :]
    gate_c[:, 1, :] = np.tile(a_A_log[h0:h0 + 2], 4)[None, :]
    return dict(w_in=w, convT=convT, gT=gT, gate_c=gate_c)


def prep_HB(b_w_in, b_norm, kv_norm, w_kv, q_gain, k_gain, sinks, rel_bias, q):
    w_qz = np.ascontiguousarray(np.concatenate([b_w_in[:, q * 512:(q + 1) * 512],
                                                b_w_in[:, 2048 + q * 512:2048 + (q + 1) * 512]], 1))
    wk = w_kv[:, q * 64:(q + 1) * 64]
    wv = w_kv[:, 256 + q * 64:256 + (q + 1) * 64]
    w_kv_c = np.ascontiguousarray(np.concatenate([wk, wk, wv], 1))
    hp_c = np.zeros((128, 8), np.float32)
    hp_c[:, 0] = np.tile(q_gain, 2)
    hp_c[:, 1] = np.tile(k_gain, 2)
    for j in range(4):
        hp_c[:64, 2 + j] = sinks[8 * q + 2 * j]
        hp_c[64:, 2 + j] = sinks[8 * q + 2 * j + 1]
    return dict(w_qz=w_qz, w_kv=w_kv_c, bgT=np.ascontiguousarray(b_norm.reshape(8, 128).T),
                kvgT=np.ascontiguousarray(kv_norm.reshape(8, 128).T), hp_c=hp_c,
                relb=np.ascontiguousarray(rel_bias[:, 8 * q:8 * q + 8]), onehot=_bucket_onehot())


def phase_OF(fw, nc, C, tag, T, xT, og_all, w_out_c, gainT, ssq_loc, ssq_all, hT_loc, groups,
             b_og, b_ssql, b_ssqa, b_hTl, on_chunk=None):
    NSB = T // 512
    do_proj = og_all is not None
    do_norm = hT_loc is not None
    big = [fw.ps(f"{tag}_big{i}", [128, 512], F32) for i in range(4)]
    if do_proj:
        Wo = fw.sb(f"{tag}_Wo", [128, 16, 256], BF16)
        wst = fw.sb(f"{tag}_wst", [128, 16, 256], F32)
        gsb = fw.sb(f"{tag}_gain", [128, 2], F32)
        ogt = [fw.sb(f"{tag}_ogt{i}", [128, 16, 512], BF16) for i in range(2)]
        fw.q_sp.dma(gsb.t[:], gainT, writes=[gsb.b])
        fw.q_pool.dma(wst.t[:], w_out_c.rearrange("(r p) f -> p r f", p=128), writes=[wst.b])
        for r in range(16):
            fw.act_(Wo.t[:, r, :], wst.t[:, r, :], AF.Copy, [wst.b, gsb.b], [Wo.b], scale=gsb.t[:, (r % 2):(r % 2) + 1])
    if do_norm:
        sq = [fw.sb(f"{tag}_sq{i}", [128, 512], BF16) for i in range(2)]
        srow = fw.sb(f"{tag}_srow", [1, T], F32)
        parts = fw.sb(f"{tag}_parts", [4, T], F32)
        lnt = [fw.sb(f"{tag}_ln{i}", [128, 512], F32) for i in range(2)]
        hsb = [fw.sb(f"{tag}_hsb{i}", [128, 2, 512], BF16) for i in range(2)]

    def load_og(sb_):
        dst = ogt[sb_ % 2]
        fw.q_sp.dma(dst.t[:], og_all[sb_].rearrange("(r p) t -> p r t", p=128), reads=[b_og[sb_]], writes=[dst.b])

    if do_proj:
        load_og(0)
    for sb_ in range(NSB):
        cols = slice(sb_ * 512, (sb_ + 1) * 512)
        if do_proj:
            if sb_ + 1 < NSB:
                load_og(sb_ + 1)
            o_ = ogt[sb_ % 2]
            for fc in range(2):
                pb = big[fc]
                for r in range(16):
                    fw.mm(pb.t[:], Wo.t[:, r, fc * 128:(fc + 1) * 128], o_.t[:, r, :], r == 0, r == 15, [Wo.b, o_.b], [pb.b])
                fw.tt(xT.t[:, fc, cols], pb.t[:], xT.t[:, fc, cols], ALU.add, [pb.b, xT.b], [xT.b])
        if do_norm:
            pb = big[2 + sb_ % 2]
            for fc in range(2):
                fw.act_(sq[fc].t[:], xT.t[:, fc, cols], AF.Square, [xT.b], [sq[fc].b])
                fw.mm(pb.t[:], C.ONESbf, sq[fc].t[:], fc == 0, fc == 1, [C.cbf.b, sq[fc].b], [pb.b])
            fw.act_(srow.t[0:1, cols], pb.t[0:1, :], AF.Copy, [pb.b], [srow.b])
    if not do_norm:
        return
    fw.q_pool.dma(ssq_loc, srow.t[:], reads=[srow.b], writes=[b_ssql])
    fw.collective("AllGather", ssq_loc, ssq_all, groups, reads=[b_ssql], writes=[b_ssqa])
    fw.q_pool.dma(parts.t[:], ssq_all, reads=[b_ssqa], writes=[parts.b])
    for sb_ in range(NSB):
        cols = slice(sb_ * 512, (sb_ + 1) * 512)
        pb, ln_, h_ = big[sb_ % 2], lnt[sb_ % 2], hsb[sb_ % 2]
        fw.mm(pb.t[:], C.ONES[0:4, :], parts.t[:, cols], True, True, [C.c32.b, parts.b], [pb.b])
        fw.act_(ln_.t[:], pb.t[:], AF.Ln, [pb.b, C.eps.b], [ln_.b], bias=C.eps.t[:], scale=1.0 / D)
        fw.act_(ln_.t[:], ln_.t[:], AF.Exp, [ln_.b], [ln_.b], scale=-0.5)
        for fc in range(2):
            fw.tt(h_.t[:, fc, :], xT.t[:, fc, cols], ln_.t[:], ALU.mult, [xT.b, ln_.b], [h_.b])
        fw.q_pool.dma(hT_loc[sb_].rearrange("(c p) t -> p c t", p=128), h_.t[:], reads=[h_.b], writes=[b_hTl[sb_]])
        if on_chunk is not None:
            on_chunk(sb_)


GROUPS = [[0, 1, 2, 3], [4, 5, 6, 7]]


def build_fused(T=SEQ):
    nc = bass.Bass("TRN2", target_bir_lowering=False)
    I = lambda n, s, d: nc.dram_tensor(n, s, d, kind="ExternalInput").ap()
    c32_ap, cbf_ap = I("c32", [128, 896], F32), I("cbf", [128, 384], BF16)
    xT_in = I("xT_in", [256, T], F32)
    A_in = [dict(w_in=I(f"a{l}_w_in", [D, 1540], F32), convT=I(f"a{l}_convT", [128, 8, 4], F32),
                 gT=I(f"a{l}_gT", [128, 8], F32), gate_c=I(f"a{l}_gate_c", [128, 2, 8], F32)) for l in range(2)]
    B_in = [dict(w_qz=I(f"b{j}_w_qz", [D, 1024], F32), bgT=I(f"b{j}_bgT", [128, 8], F32),
                 hp_c=I(f"b{j}_hp_c", [128, 8], F32)) for j in range(2)]
    w_kv, kvgT = I("w_kv_c", [D, 192], F32), I("kvgT", [128, 8], F32)
    relb, onehot = I("relb", [32, 8], F32), I("onehot", [32, 128], F32)
    Wout = [I(f"w_out{l}", [2048, 256], F32) for l in range(4)]
    gainT = [I(f"gainT{l}", [128, 2], F32) for l in range(4)]
    xT_out = nc.dram_tensor("xT_out", [256, T], F32, kind="ExternalOutput").ap()
    Dr = lambda n, s, d: nc.dram_tensor(n, s, d).ap()
    ssq_loc = [Dr(f"ssq_loc{l}", [1, T], F32) for l in range(4)]
    ssq_all = [Dr(f"ssq_all{l}", [4, T], F32) for l in range(4)]
    NCH = T // 512
    hT_loc = [[Dr(f"hT_loc{l}_{i}", [256, 512], BF16) for i in range(NCH)] for l in range(4)]
    hT_all = [[Dr(f"hT_all{l}_{i}", [1024, 512], BF16) for i in range(NCH)] for l in range(4)]
    ogT = [[Dr(f"ogT{l}_{i}", [512, 512], BF16) for i in range(NCH)] for l in range(4)]
    og_all = [[Dr(f"og_all{l}_{i}", [2048, 512], BF16) for i in range(NCH)] for l in range(4)]
    scratch = [Dr(f"scr{j}", [8 * 128 * 385 + 1024], F32) for j in range(2)]
    kv_dram = dict(ktab=Dr("kv_ktab", [128, T + 128], BF16), vpad=Dr("kv_vpad", [128, (T // 128 + 1) * 192], BF16),
                   ebt=Dr("kv_ebt", [128, 2048], BF16), buf=Buf("kvdram"), buf_ebt=Buf("ebtdram"))
    with ExitStack() as st:
        fw = FW(nc, st)
        C = Consts(fw, nc, c32_ap, cbf_ap)
        xT = fw.sb("xT_res", [128, 2, T], F32)
        fw.q_sp.dma(xT.t[:], xT_in.rearrange("(c p) t -> p c t", p=128), writes=[xT.b])
        b_hTall = [[Buf(f"hTall{l}_{i}") for i in range(NCH)] for l in range(4)]
        b_ogall = [[Buf(f"ogall{l}_{i}") for i in range(NCH)] for l in range(4)]

        def of_phase(l, proj_from):
            with ExitStack() as sc:
                fw.scope = sc
                b1, b2 = Buf("ssql"), Buf("ssqa")
                b3 = [Buf(f"hTl{i}") for i in range(NCH)]
                last = (l == 4)

                def oc(i):
                    fw.collective("AllGather", hT_loc[l][i], hT_all[l][i], GROUPS, reads=[b3[i]], writes=[b_hTall[l][i]])

                phase_OF(fw, nc, C, f"OF{l}", T, xT,
                         og_all[proj_from] if proj_from is not None else None,
                         Wout[proj_from] if proj_from is not None else None,
                         gainT[proj_from] if proj_from is not None else None,
                         None if last else ssq_loc[l], None if last else ssq_all[l], None if last else hT_loc[l], GROUPS,
                         b_ogall[proj_from] if proj_from is not None else None, b1, b2, b3, on_chunk=oc)
                fw.barrier()
            fw.scope = st

        def h_phase(l):
            with ExitStack() as sc:
                fw.scope = sc
                b_o = [Buf(f"ogT{i}") for i in range(NCH)]

                def oc(i):
                    fw.collective("AllGather", ogT[l][i], og_all[l][i], GROUPS, reads=[b_o[i]], writes=[b_ogall[l][i]])

                if l < 2:
                    a = A_in[l]
                    phase_H(fw, nc, C, f"HA{l}", hT_all[l], T, a["w_in"], a["convT"], a["gT"], a["gate_c"], ogT[l],
                            b_hTall[l], b_o, on_chunk=oc)
                else:
                    bb = B_in[l - 2]
                    phase_HB(fw, nc, C, f"HB{l}", T, hT_all[l], hT_all[2], bb["w_qz"], w_kv, bb["bgT"], kvgT, bb["hp_c"], relb,
                             onehot, scratch[l - 2], ogT[l], b_hTall[l], b_hTall[2], b_o, on_chunk=oc,
                             kv_dram=kv_dram, kv_load=(l == 3))
                fw.barrier()
            fw.scope = st

        with ExitStack() as sc0:
            fw.scope = sc0
            build_ebt(fw, nc, C, "EB", relb, onehot, scratch[0], kv_dram["ebt"], kv_dram["buf_ebt"])
            fw.barrier()
        fw.scope = st
        of_phase(0, None)
        for l in range(4):
            h_phase(l)
            of_phase(l + 1, l)
        b_out = Buf("xout")
        fw.q_pool.dma(xT_out.rearrange("(c p) t -> p c t", p=128), xT.t[:], reads=[xT.b], writes=[b_out])
        fw.finish([b_out])
        build_fused.counts = {e.name: e.ninst for e in (fw.pe, fw.dve, fw.act, fw.pool, fw.sp)}
    return nc


def make_in_maps(inputs, T=SEQ):
    f = lambda a: np.ascontiguousarray(np.asarray(a, dtype=np.float32))
    p = {k: f(v) for k, v in inputs.items()}
    c32, cbf = const_tables()
    oh = _bucket_onehot()
    ones_gain = np.ones((128, 2), np.float32)
    maps = []
    for c in range(8):
        b, q = c // 4, c % 4
        m = dict(c32=c32, cbf=cbf, onehot=oh)
        m["xT_in"] = np.ascontiguousarray(p["x"][b, :T, q * 256:(q + 1) * 256].T)
        for l in range(2):
            d = prep_HA(p["a_w_in"][l], p["a_conv"][l], p["a_norm"][l], p["a_A_log"][l], p["a_dt_bias"][l], q)
            for k, v in d.items():
                m[f"a{l}_{k}"] = v
            m[f"w_out{l}"] = np.ascontiguousarray(p["a_w_out"][l][:, q * 256:(q + 1) * 256])
            m[f"gainT{l}"] = np.ascontiguousarray(p["a_o_gain"][l].reshape(2, 128).T)
        for j in range(2):
            d = prep_HB(p["b_w_in"][j], p["b_norm"][j], p["kv_norm"], p["w_kv"], p["b_q_gain"][j], p["k_gain"],
                        p["b_sinks"][j], p["rel_bias"], q)
            m[f"b{j}_w_qz"], m[f"b{j}_bgT"], m[f"b{j}_hp_c"] = d["w_qz"], d["bgT"], d["hp_c"]
            m["w_kv_c"], m["kvgT"], m["relb"] = d["w_kv"], d["kvgT"], d["relb"]
            m[f"w_out{2 + j}"] = np.ascontiguousarray(p["b_w_out"][j][:, q * 256:(q + 1) * 256])
            m[f"gainT{2 + j}"] = ones_gain
        maps.append(m)
    return maps


_NC = {}


def kernel(**inputs):
    if "nc" not in _NC:
        _NC["nc"] = build_fused(SEQ)
    res = run_bass_kernel_spmd(_NC["nc"], make_in_maps(inputs, SEQ), core_ids=list(range(8)))
    out = np.zeros((NB, SEQ, D), np.float32)
    for c in range(8):
        b, q = c // 4, c % 4
        out[b, :, q * 256:(q + 1) * 256] = np.asarray(res.results[c]["xT_out"]).T
    return out
```
